# Optimizing a Trainium2 kernel written in Bass

```python
import math
import jax, jax.numpy as jnp
from jax import lax
import numpy as np

D_MODEL = 1024
BATCH = 16
SEQ = 4096
DEPTH = 4

HEAD_DIM = 64
MIX_WIDTH = D_MODEL
DIFF_WIDTH = MIX_WIDTH // 2
DIFF_HEADS = DIFF_WIDTH // (2 * HEAD_DIM)
SWA_WIDTH = MIX_WIDTH - DIFF_WIDTH
SWA_Q_HEADS = SWA_WIDTH // HEAD_DIM
SWA_KV_HEADS = 2
SWA_GROUP = SWA_Q_HEADS // SWA_KV_HEADS
WINDOW = 128
BLOCK = 128
D_FF = ((8 * D_MODEL // 3 + 127) // 128) * 128
ROPE_THETA = 10000.0
ALPHA = (2.0 * DEPTH) ** 0.25
BETA = (8.0 * DEPTH) ** -0.25
LN_EPS = 1e-5
RMS_EPS = 1e-5

DQ = DIFF_HEADS * 2 * HEAD_DIM
DK = DIFF_HEADS * 2 * HEAD_DIM
DV = DIFF_HEADS * 2 * HEAD_DIM
SQ = SWA_Q_HEADS * HEAD_DIM
SK = SWA_KV_HEADS * HEAD_DIM
SV = SWA_KV_HEADS * HEAD_DIM
IN_COLS = DQ + DK + DV + SQ + SK + SV

kernel_name = "hymba_diffattn_swa_macaron_deepnorm_encoder"


def layer_norm(x, g, b):
    xf = x.astype(jnp.float32)
    mu = jnp.mean(xf, axis=-1, keepdims=True)
    var = jnp.mean(jnp.square(xf - mu), axis=-1, keepdims=True)
    y = (xf - mu) * lax.rsqrt(var + LN_EPS) * g.astype(jnp.float32) + b.astype(jnp.float32)
    return y.astype(x.dtype)


def swiglu(x, w_in, w_out):
    h = x @ w_in
    gate, up = h[..., :D_FF], h[..., D_FF:]
    return (jax.nn.silu(gate) * up) @ w_out


def rope_tables(positions):
    inv = jnp.power(ROPE_THETA, -jnp.arange(0, HEAD_DIM, 2, dtype=jnp.float32) / HEAD_DIM)
    ang = positions.astype(jnp.float32)[..., None] * inv
    ang = jnp.concatenate([ang, ang], axis=-1)[:, :, None, :]
    return jnp.cos(ang), jnp.sin(ang)


def apply_rope(t, cos, sin):
    half = HEAD_DIM // 2
    t1, t2 = t[..., :half], t[..., half:]
    rot = jnp.concatenate([-t2, t1], axis=-1)
    return (t * cos.astype(t.dtype) + rot * sin.astype(t.dtype)).astype(t.dtype)


def diff_attention(q, k, v, lam, subln_g, lambda_init):
    B, S = q.shape[0], q.shape[1]
    nb = S // BLOCK
    scale = HEAD_DIM ** -0.5
    qb = q.reshape(B, nb, BLOCK, DIFF_HEADS, 2, HEAD_DIM).transpose(1, 0, 2, 3, 4, 5)

    def one_block(q_blk):
        s = jnp.einsum('bqhcd,bkhcd->bhcqk', q_blk, k).astype(jnp.float32) * scale
        p = jax.nn.softmax(s, axis=-1)
        a = p[:, :, 0] - lam * p[:, :, 1]
        return jnp.einsum('bhqk,bkhe->bqhe', a.astype(v.dtype), v)

    o = lax.map(one_block, qb)
    o = o.transpose(1, 0, 2, 3, 4).reshape(B, S, DIFF_HEADS, 2 * HEAD_DIM)
    of = o.astype(jnp.float32)
    of = of * lax.rsqrt(jnp.mean(jnp.square(of), axis=-1, keepdims=True) + RMS_EPS)
    of = of * subln_g.astype(jnp.float32) * (1.0 - lambda_init)
    return of.astype(v.dtype).reshape(B, S, DIFF_WIDTH)


def band_mask(S):
    nb = S // BLOCK
    qpos = jnp.arange(nb)[:, None, None] * BLOCK + jnp.arange(BLOCK)[None, :, None]
    kpos = (jnp.arange(nb)[:, None, None] - 1) * BLOCK + jnp.arange(3 * BLOCK)[None, None, :]
    return (jnp.abs(qpos - kpos) <= WINDOW) & (kpos >= 0) & (kpos < S)


def swa_sink_attention(q, k, v, sink, mask):
    B, S = q.shape[0], q.shape[1]
    nb = S // BLOCK
    scale = HEAD_DIM ** -0.5
    qb = q.reshape(B, nb, BLOCK, SWA_KV_HEADS, SWA_GROUP, HEAD_DIM)

    def band(t):
        tp = jnp.pad(t, ((0, 0), (BLOCK, BLOCK), (0, 0), (0, 0)))
        tp = tp.reshape(B, nb + 2, BLOCK, SWA_KV_HEADS, HEAD_DIM)
        return jnp.concatenate([tp[:, :-2], tp[:, 1:-1], tp[:, 2:]], axis=2)

    kb, vb = band(k), band(v)
    s = jnp.einsum('bnqhgd,bnkhd->bnhgqk', qb, kb).astype(jnp.float32) * scale
    s = jnp.where(mask[None, :, None, None, :, :], s, -jnp.inf)
    sk = sink.astype(jnp.float32).reshape(1, 1, SWA_KV_HEADS, SWA_GROUP, 1, 1)
    m = jnp.maximum(jnp.max(s, axis=-1, keepdims=True), sk)
    e = jnp.exp(s - m)
    p = e / (jnp.sum(e, axis=-1, keepdims=True) + jnp.exp(sk - m))
    o = jnp.einsum('bnhgqk,bnkhd->bnqhgd', p.astype(v.dtype), vb)
    return o.reshape(B, S, SWA_WIDTH)


def hybrid_mixer(x, cos, sin, mask, w_in, w_out, lam_vec, subln_g, sink, lambda_init):
    B, S = x.shape[0], x.shape[1]
    h = x @ w_in
    o0 = 0
    dq = h[..., o0:o0 + DQ]; o0 += DQ
    dk = h[..., o0:o0 + DK]; o0 += DK
    dv = h[..., o0:o0 + DV]; o0 += DV
    sq = h[..., o0:o0 + SQ]; o0 += SQ
    sk = h[..., o0:o0 + SK]; o0 += SK
    sv = h[..., o0:o0 + SV]

    dq = apply_rope(dq.reshape(B, S, 2 * DIFF_HEADS, HEAD_DIM), cos, sin)
    dk = apply_rope(dk.reshape(B, S, 2 * DIFF_HEADS, HEAD_DIM), cos, sin)
    dq = dq.reshape(B, S, DIFF_HEADS, 2, HEAD_DIM)
    dk = dk.reshape(B, S, DIFF_HEADS, 2, HEAD_DIM)
    dv = dv.reshape(B, S, DIFF_HEADS, 2 * HEAD_DIM)
    lv = lam_vec.astype(jnp.float32)
    lam = jnp.exp(jnp.sum(lv[0] * lv[1])) - jnp.exp(jnp.sum(lv[2] * lv[3])) + lambda_init
    y_diff = diff_attention(dq, dk, dv, lam, subln_g, lambda_init)

    sq = apply_rope(sq.reshape(B, S, SWA_Q_HEADS, HEAD_DIM), cos, sin)
    sk = apply_rope(sk.reshape(B, S, SWA_KV_HEADS, HEAD_DIM), cos, sin)
    sv = sv.reshape(B, S, SWA_KV_HEADS, HEAD_DIM)
    y_swa = swa_sink_attention(sq, sk, sv, sink, mask)

    return jnp.concatenate([y_diff, y_swa], axis=-1) @ w_out


def setup_inputs(seed: int = 0) -> dict:
    key = jax.random.key(seed)
    ks = jax.random.split(key, 14)
    f32 = jnp.float32
    x = jax.random.normal(ks[0], (BATCH, SEQ, D_MODEL), f32)
    positions = jnp.broadcast_to(jnp.arange(SEQ, dtype=jnp.int32)[None, :], (BATCH, SEQ))
    col_scale = jnp.concatenate([
        jnp.ones((DQ + DK,), f32), jnp.full((DV,), BETA, f32),
        jnp.ones((SQ + SK,), f32), jnp.full((SV,), BETA, f32)])
    w_in = jax.random.normal(ks[1], (DEPTH, D_MODEL, IN_COLS), f32) * (D_MODEL ** -0.5) * col_scale
    w_out = jax.random.normal(ks[2], (DEPTH, MIX_WIDTH, D_MODEL), f32) * (MIX_WIDTH ** -0.5) * BETA
    diff_lambda = jax.random.normal(ks[3], (DEPTH, 4, HEAD_DIM), f32) * 0.1
    diff_subln_g = 1.0 + 0.02 * jax.random.normal(ks[4], (DEPTH, 2 * HEAD_DIM), f32)
    swa_sink = 0.5 * jax.random.normal(ks[5], (DEPTH, SWA_Q_HEADS), f32)
    ffn1_w_in = jax.random.normal(ks[6], (DEPTH, D_MODEL, 2 * D_FF), f32) * (D_MODEL ** -0.5)
    ffn1_w_out = jax.random.normal(ks[7], (DEPTH, D_FF, D_MODEL), f32) * (D_FF ** -0.5) * BETA
    ffn2_w_in = jax.random.normal(ks[8], (DEPTH, D_MODEL, 2 * D_FF), f32) * (D_MODEL ** -0.5)
    ffn2_w_out = jax.random.normal(ks[9], (DEPTH, D_FF, D_MODEL), f32) * (D_FF ** -0.5) * BETA
    ln_g = 1.0 + 0.02 * jax.random.normal(ks[10], (DEPTH, 3, D_MODEL), f32)
    ln_b = 0.02 * jax.random.normal(ks[11], (DEPTH, 3, D_MODEL), f32)
    return {"x": x, "positions": positions, "w_in": w_in, "w_out": w_out,
            "diff_lambda": diff_lambda, "diff_subln_g": diff_subln_g, "swa_sink": swa_sink,
            "ffn1_w_in": ffn1_w_in, "ffn1_w_out": ffn1_w_out,
            "ffn2_w_in": ffn2_w_in, "ffn2_w_out": ffn2_w_out,
            "ln_g": ln_g, "ln_b": ln_b}


def reference(x, positions, w_in, w_out, diff_lambda, diff_subln_g, swa_sink,
              ffn1_w_in, ffn1_w_out, ffn2_w_in, ffn2_w_out, ln_g, ln_b):
    S = x.shape[1]
    cos, sin = rope_tables(positions)
    mask = band_mask(S)
    for l in range(DEPTH):
        lambda_init = 0.8 - 0.6 * math.exp(-0.3 * l)
        x = layer_norm(ALPHA * x + 0.5 * swiglu(x, ffn1_w_in[l], ffn1_w_out[l]), ln_g[l, 0], ln_b[l, 0])
        mix = hybrid_mixer(x, cos, sin, mask, w_in[l], w_out[l], diff_lambda[l],
                           diff_subln_g[l], swa_sink[l], lambda_init)
        x = layer_norm(ALPHA * x + mix, ln_g[l, 1], ln_b[l, 1])
        x = layer_norm(ALPHA * x + 0.5 * swiglu(x, ffn2_w_in[l], ffn2_w_out[l]), ln_g[l, 2], ln_b[l, 2])
    return x
```

```python
from contextlib import ExitStack
import math
import numpy as np
import concourse.bass as bass
import concourse.mybir as mybir
from concourse.bass_utils import run_bass_kernel_spmd

F32 = mybir.dt.float32
BF16 = mybir.dt.bfloat16
I32 = mybir.dt.int32
U8 = mybir.dt.uint8
AF = mybir.ActivationFunctionType
ALU = mybir.AluOpType
AX = mybir.AxisListType

D = 1024
SEQ = 4096
DEPTH = 4
DFF = 2816
NF = DFF // 128
INC = 2304
ALPHA = (2.0 * DEPTH) ** 0.25
LN_EPS = 1e-5
RMS_EPS = 1e-5
SCALE = 0.125
NCORES = 8
SEM_WRAP = 30000
FILLER = True
ARENA_BASE = 16512
ARENA_SIZE = 212000


class Buf:
    __slots__ = ("name", "last_w", "readers", "dma_readers", "sem", "dma_total", "last_dma")

    def __init__(self, name):
        self.name = name
        self.last_w = None
        self.readers = {}
        self.dma_readers = []
        self.sem = {}
        self.dma_total = 0
        self.last_dma = {}


class Op:
    __slots__ = ("idx", "eng", "fn", "deps", "signaled", "ev", "is_dma", "dbuf", "ndma")


class PhysSem:
    __slots__ = ("handle", "total")

    def __init__(self):
        self.handle = None
        self.total = 0


class Sched:
    ENGS = ("pe", "act", "dve", "pool", "sp")

    def __init__(self, nc):
        self.nc = nc
        self.ops = []
        self.es = ExitStack()
        self.phys = []
        self.free_phys = {"hw": [], "sw": []}
        self.active_bufs = []
        self.last_on = {}
        self.dmas_since = []

    def buf(self, name):
        return Buf(name)

    def add(self, eng, fn, reads=(), writes=(), dma=None, ndma=1, extra=()):
        op = Op()
        op.idx = len(self.ops)
        op.eng = eng
        op.fn = fn
        op.is_dma = dma is not None
        op.dbuf = dma
        op.ndma = ndma
        op.signaled = op.is_dma
        op.ev = None
        hard = set()
        soft = set()
        for b in tuple(reads) + tuple(writes):
            if b.last_w is not None:
                hard.add(b.last_w)
        for b in writes:
            for r in b.readers.values():
                soft.add(r)
            for r in b.dma_readers:
                hard.add(r)
        for e in extra:
            if e is not None:
                hard.add(e)
        if op.is_dma:
            qc = "sw" if eng == "pool" else "hw"
            if dma.last_dma.get(qc) is not None:
                hard.add(dma.last_dma[qc])
            dma.last_dma[qc] = op
            if dma.sem.get(qc) is None:
                fp = self.free_phys[qc]
                while fp and fp[-1].total > 20000:
                    fp.pop()
                if fp:
                    dma.sem[qc] = fp.pop()
                else:
                    dma.sem[qc] = PhysSem()
                    self.phys.append(dma.sem[qc])
                self.active_bufs.append((dma, qc))
            ps_ = dma.sem[qc]
            ps_.total += 16 * ndma
            op.ev = (ps_, ps_.total)
            op.dbuf = ps_
            self.dmas_since.append(op)
        deps = set()
        for d in hard | soft:
            if d is op:
                continue
            if (not d.is_dma) and (not op.is_dma) and d.eng == eng:
                if eng == "pe":
                    continue
                if d not in hard:
                    continue
            deps.add(d)
        for d in deps:
            d.signaled = True
        op.deps = deps
        for b in writes:
            b.last_w = op
            b.readers = {}
            b.dma_readers = []
        ws = set(id(b) for b in writes)
        for b in reads:
            if id(b) in ws:
                continue
            if op.is_dma:
                b.dma_readers.append(op)
            else:
                b.readers[eng] = op
        self.ops.append(op)
        if fn is not None and not op.is_dma:
            self.last_on[eng] = op
        return op

    def barrier(self):
        deps = list(self.last_on.values()) + list(self.dmas_since)
        self.dmas_since = []
        for e in self.ENGS:
            self.add(e, None, extra=deps)
        for b, qc in self.active_bufs:
            self.free_phys[qc].append(b.sem[qc])
            b.sem[qc] = None
            b.last_dma[qc] = None
        self.active_bufs = []

    def emit(self):
        nc = self.nc
        es = self.es
        cnt = {e: 0 for e in self.ENGS}
        for op in self.ops:
            if not op.is_dma and op.signaled:
                cnt[op.eng] += 1
        import os
        if os.environ.get("KDBG"):
            print("signaled", cnt, "nphys", len(self.phys), "max_total", max(p.total for p in self.phys), "nops", len(self.ops), flush=True)
        esems = {}
        for e in self.ENGS:
            n = cnt[e] // SEM_WRAP + 1
            esems[e] = [es.enter_context(nc.semaphore(f"s_{e}_{i}")) for i in range(n)]
        for i, b in enumerate(self.phys):
            b.handle = es.enter_context(nc.semaphore(f"d{i}"))
        c = {e: 0 for e in self.ENGS}
        for op in self.ops:
            if not op.is_dma and op.signaled:
                k = c[op.eng]
                c[op.eng] += 1
                op.ev = (op.eng, k // SEM_WRAP, k % SEM_WRAP + 1)
        per = {e: [] for e in self.ENGS}
        for op in self.ops:
            per[op.eng].append(op)

        def run(eng_name, eng):
            waited_c = {}
            waited_d = {}
            for op in per[eng_name]:
                need_c = {}
                need_d = {}
                for d in op.deps:
                    if d.is_dma:
                        b, v = d.ev
                        if waited_d.get(id(b), 0) >= v:
                            continue
                        if need_d.get(id(b), (None, 0))[1] < v:
                            need_d[id(b)] = (b, v)
                    else:
                        if d.ev is None:
                            continue
                        pe_, k, v = d.ev
                        if waited_c.get(pe_, (-1, 0)) >= (k, v):
                            continue
                        if need_c.get(pe_, (-1, 0)) < (k, v):
                            need_c[pe_] = (k, v)
                for pe_, (k, v) in need_c.items():
                    eng.wait_ge(esems[pe_][k], v)
                    waited_c[pe_] = (k, v)
                for _, (b, v) in need_d.items():
                    eng.wait_ge(b.handle, v)
                    waited_d[id(b)] = v
                if op.fn is None:
                    continue
                r = op.fn(eng)
                if op.is_dma:
                    if not isinstance(r, (list, tuple)):
                        r = [r]
                    assert len(r) == op.ndma, (len(r), op.ndma)
                    for ins in r:
                        ins.then_inc(op.dbuf.handle, 16)
                elif op.signaled:
                    _, k, v = op.ev
                    r.then_inc(esems[op.eng][k], 1)

        with nc.Block() as block:
            @block.tensor
            def _(e):
                run("pe", e)

            @block.scalar
            def _(e):
                run("act", e)

            @block.vector
            def _(e):
                run("dve", e)

            @block.gpsimd
            def _(e):
                run("pool", e)

            @block.sync
            def _(e):
                run("sp", e)

    def close(self):
        self.es.close()


DT_SIZE = {F32: 4, BF16: 2, I32: 4, U8: 1}


class Layout:
    cnt = 0

    def __init__(self, nc, start):
        self.nc = nc
        self.off = start

    def t(self, name, shape, dtype):
        nbytes = DT_SIZE[dtype]
        for s in shape[1:]:
            nbytes *= s
        off = (self.off + 63) // 64 * 64
        self.off = off + nbytes
        assert self.off <= ARENA_SIZE, (name, self.off)
        Layout.cnt += 1
        h = self.nc.alloc_sbuf_tensor_at(f"{name}_{Layout.cnt}", list(shape), dtype, offset=ARENA_BASE + off)
        return h, Buf(name)


def bcast_cols(ap, n):
    return bass.AP(ap.tensor, ap.offset, [list(ap.ap[0]), [0, n]])


def bcast_last(ap, n):
    return bass.AP(ap.tensor, ap.offset, [list(x) for x in ap.ap] + [[0, n]])


def build(NSEQ=2, depth=DEPTH, stop=None, dbg=False, wdepth=DEPTH):
    nc = bass.Bass("TRN2", target_bir_lowering=False)
    NTOK = NSEQ * SEQ
    NTT = NTOK // 512
    ein = "ExternalInput"

    def dram(name, shape, dt, kind="Internal"):
        return nc.dram_tensor(name, list(shape), dt, kind=kind)

    x_in = dram("x", [NTOK, D], F32, ein).ap()
    pos_d = dram("positions", [NSEQ, SEQ], I32, ein).ap()
    w_in_d = dram("w_in", [wdepth, D, INC], F32, ein).ap()
    w_out_d = dram("w_out", [wdepth, D, D], F32, ein).ap()
    dlam_d = dram("diff_lambda", [1, DEPTH * 4 * 64], F32, ein).ap()
    subg_d = dram("diff_subln_g", [1, DEPTH * 128], F32, ein).ap()
    sink_d = dram("swa_sink", [1, DEPTH * 8], F32, ein).ap()
    f1i_d = dram("ffn1_w_in", [wdepth, D, 2 * DFF], F32, ein).ap()
    f1o_d = dram("ffn1_w_out", [wdepth, DFF, D], F32, ein).ap()
    f2i_d = dram("ffn2_w_in", [wdepth, D, 2 * DFF], F32, ein).ap()
    f2o_d = dram("ffn2_w_out", [wdepth, DFF, D], F32, ein).ap()
    lng_d = dram("ln_g", [DEPTH * 3, D], F32, ein).ap()
    lnb_d = dram("ln_b", [DEPTH * 3, D], F32, ein).ap()
    ident_d = dram("c_ident", [128, 128], F32, ein).ap()
    inv_d = dram("c_inv", [128, 1], F32, ein).ap()
    mask_d = dram("c_mask", [128, 2 * 3 * 128], F32, ein).ap()
    out_d = dram("out", [NTOK, D], F32, "ExternalOutput").ap()

    sk = "ExternalOutput" if dbg else "Internal"
    XS = dram("xs", [NTOK, D], F32).ap()
    QKT = dram("qkt", [14, 128, NTOK], BF16, sk).ap()
    VS = dram("vs", [NTOK, 640], BF16, sk).ap()
    YT = dram("yt", [8, 128, NTOK], BF16, sk).ap()
    ROPE = dram("rope", [NSEQ, 2, 128, SEQ], F32, sk).ap()

    S = Sched(nc)
    arena = nc.alloc_sbuf_tensor("arena", [128, ARENA_SIZE], U8)
    assert nc.lookup_mloc(arena).addr == ARENA_BASE, nc.lookup_mloc(arena).addr
    PS = nc.alloc_psum_tensor("ps", [128, 8, 512], F32)
    PSB = PS.bitcast(BF16)
    psb = [Buf(f"bank{i}") for i in range(8)]

    xin_b = Buf("xin")
    XIN_b = [[xin_b] * 4 for t in range(NTT)]
    XS_b = [[Buf(f"xs{t}_{j}") for j in range(4)] for t in range(NTT)]
    OUT_b = [[Buf(f"out{t}_{j}") for j in range(4)] for t in range(NTT)]
    QKT_b = [[Buf(f"qkt{c}_{t}") for t in range(NTT)] for c in range(14)]
    VS_b = [Buf(f"vs{t}") for t in range(NTT)]
    YT_b = [[Buf(f"yt{c}_{t}") for t in range(NTT)] for c in range(8)]
    ROPE_b = [Buf(f"rope{s}") for s in range(NSEQ)]
    w_b = Buf("weights_dram")
    out_b = Buf("out")
    out_stores = []

    L0 = Layout(nc, 0)
    ident, ident_b = L0.t("ident", [128, 128], BF16)
    negh, negh_b = L0.t("negh", [128, 1], F32)
    invt, invt_b = L0.t("invt", [128, 1], F32)
    gb, gb_b = L0.t("gb", [128, 2, D], F32)
    neglam, neglam_b = L0.t("neglam", [128, DEPTH], F32)
    gsub, gsub_b = L0.t("gsub", [128, DEPTH, 128], F32)
    esink, esink_b = L0.t("esink", [128, DEPTH * 8], F32)
    stt = [L0.t(f"st{i}", [128, 32], F32) for i in range(2)]
    COMMON_END = L0.off

    S.add("pool", lambda e: e.dma_start(out=ident[:], in_=ident_d), reads=[w_b], writes=[ident_b], dma=ident_b)
    S.add("sp", lambda e: e.dma_start(out=invt[:], in_=inv_d), reads=[w_b], writes=[invt_b], dma=invt_b)
    S.add("dve", lambda e: e.memset(negh[:], -0.5), writes=[negh_b])

    def phase_scalars():
        L = Layout(nc, COMMON_END)
        dl, dl_b = L.t("dl", [128, DEPTH * 256], F32)
        prod, prod_b = L.t("prod", [128, DEPTH * 128], F32)
        sums, sums_b = L.t("sums", [128, DEPTH * 2], F32)
        ee, ee_b = L.t("ee", [128, DEPTH * 2], F32)
        dd, dd_b = L.t("dd", [128, DEPTH], F32)
        sg, sg_b = L.t("sg", [128, DEPTH * 128], F32)
        sk_, sk_b = L.t("sk", [128, DEPTH * 8], F32)
        S.add("sp", lambda e: e.dma_start(out=dl[:], in_=dlam_d.broadcast_to([128, DEPTH * 256])),
              reads=[w_b], writes=[dl_b], dma=dl_b)
        S.add("sp", lambda e: e.dma_start(out=sg[:], in_=subg_d.broadcast_to([128, DEPTH * 128])),
              reads=[w_b], writes=[sg_b], dma=sg_b)
        S.add("sp", lambda e: e.dma_start(out=sk_[:], in_=sink_d.broadcast_to([128, DEPTH * 8])),
              reads=[w_b], writes=[sk_b], dma=sk_b)
        dl5 = dl[:].rearrange("p (a w d) -> p a w d", w=2, d=64)
        S.add("dve", lambda e: e.tensor_tensor(prod[:].rearrange("p (a d) -> p a d", d=64),
                                               dl5[:, :, 0, :], dl5[:, :, 1, :], ALU.mult),
              reads=[dl_b], writes=[prod_b])
        S.add("dve", lambda e: e.reduce_sum(sums[:], prod[:].rearrange("p (a d) -> p a d", d=64), axis=AX.X),
              reads=[prod_b], writes=[sums_b])
        S.add("act", lambda e: e.activation(ee[:], sums[:], AF.Exp), reads=[sums_b], writes=[ee_b])
        ee3 = ee[:].rearrange("p (l c) -> p l c", c=2)
        S.add("dve", lambda e: e.tensor_tensor(dd[:], ee3[:, :, 1], ee3[:, :, 0], ALU.subtract),
              reads=[ee_b], writes=[dd_b])
        for l in range(DEPTH):
            li = 0.8 - 0.6 * math.exp(-0.3 * l)
            S.add("dve", lambda e, l=l, li=li: e.tensor_scalar(neglam[:, l:l + 1], dd[:, l:l + 1], -li, None, ALU.add),
                  reads=[dd_b], writes=[neglam_b])
            S.add("act", lambda e, l=l, li=li: e.mul(gsub[:, l, :], sg[:, l * 128:(l + 1) * 128], 1.0 - li),
                  reads=[sg_b], writes=[gsub_b])
        S.add("act", lambda e: e.activation(esink[:], sk_[:], AF.Exp), reads=[sk_b], writes=[esink_b])
        return L.off

    def phase_rope(start):
        L = Layout(nc, start)
        posi, posi_b = L.t("posi", [128, SEQ], I32)
        tt_, tt_b = L.t("rt", [128, SEQ], F32)
        t2, t2_b = L.t("rt2", [128, SEQ], F32)
        ti, ti_b = L.t("rti", [128, SEQ], I32)
        tf, tf_b = L.t("rtf", [128, SEQ], F32)
        res = [L.t(f"rres{i}", [128, SEQ], F32) for i in range(2)]
        for s in range(NSEQ):
            S.add("sp", lambda e, s=s: [e.dma_start(out=posi[:, i * 512:(i + 1) * 512],
                                                    in_=pos_d[s:s + 1, i * 512:(i + 1) * 512].broadcast_to([128, 512])) for i in range(8)],
                  reads=[w_b], writes=[posi_b], dma=posi_b, ndma=8)
            S.add("dve", lambda e: e.tensor_copy(tt_[:], posi[:]), reads=[posi_b], writes=[tt_b])
            S.add("dve", lambda e: e.tensor_scalar(tt_[:], tt_[:], invt[:, 0:1], 1.0 / (2.0 * math.pi), ALU.mult, ALU.mult),
                  reads=[tt_b, invt_b], writes=[tt_b])
            for which in (1, 0):
                if which == 0:
                    S.add("dve", lambda e: e.tensor_scalar(t2[:], tt_[:], 0.25, None, ALU.add), reads=[tt_b], writes=[t2_b])
                    src, src_b = t2, t2_b
                else:
                    src, src_b = tt_, tt_b
                S.add("dve", lambda e, src=src: e.tensor_copy(ti[:], src[:]), reads=[src_b], writes=[ti_b])
                S.add("dve", lambda e: e.tensor_copy(tf[:], ti[:]), reads=[ti_b], writes=[tf_b])
                S.add("dve", lambda e, src=src: e.tensor_tensor(tf[:], src[:], tf[:], ALU.subtract),
                      reads=[src_b, tf_b], writes=[tf_b])
                r, r_b = res[which]
                S.add("act", lambda e, r=r: e.activation(r[:], tf[:], AF.Sin, scale=6.28318), reads=[tf_b], writes=[r_b])
                S.add("sp", lambda e, r=r, s=s, which=which: e.dma_start(out=ROPE[s, which], in_=r[:]),
                      reads=[r_b], writes=[ROPE_b[s]], dma=r_b)

    def ln_epilogue_a(y_ap, y_bufs, coef, xres, xres_b, z, z_b, st, st_b):
        S.add("dve", lambda e: e.scalar_tensor_tensor(z[:].rearrange("p (a b) -> p a b", a=2), y_ap, coef / ALPHA,
                                                      xres[:].rearrange("p (a b) -> p a b", a=2), ALU.mult, ALU.add),
              reads=list(y_bufs) + [xres_b], writes=[z_b])
        for hh in range(2):
            S.add("dve", lambda e, hh=hh: e.bn_stats(st[:, hh * 6:(hh + 1) * 6], z[:, hh * 512:(hh + 1) * 512]),
                  reads=[z_b], writes=[st_b])
        S.add("dve", lambda e: e.bn_aggr(st[:, 12:14], st[:, 0:12].rearrange("p (a b) -> p a b", a=2)),
              reads=[st_b], writes=[st_b])
        S.add("dve", lambda e: e.tensor_scalar(st[:, 14:15], st[:, 13:14], LN_EPS / (ALPHA * ALPHA), None, ALU.add),
              reads=[st_b], writes=[st_b])
        S.add("pool", lambda e: e.tensor_tensor(st[:, 15:16], st[:, 14:15], negh[:], ALU.pow),
              reads=[st_b, negh_b], writes=[st_b])

    def ln_epilogue_b1(z, z_b, st, st_b, zn_eng="dve"):
        S.add("dve", lambda e: e.scalar_tensor_tensor(st[:, 16:17], st[:, 12:13], -1.0, st[:, 15:16], ALU.mult, ALU.mult),
              reads=[st_b], writes=[st_b])
        if zn_eng == "dve":
            S.add("dve", lambda e: e.tensor_scalar(z[:], z[:], st[:, 15:16], st[:, 16:17], ALU.mult, ALU.add),
                  reads=[z_b, st_b], writes=[z_b])
        else:
            S.add("act", lambda e: e.activation(z[:], z[:], AF.Identity, bias=st[:, 16:17], scale=st[:, 15:16]),
                  reads=[z_b, st_b], writes=[z_b])

    def ln_epilogue_b2(z, z_b, dst_ap, dst_buf, is_out, g_eng="pool", b_eng="pool"):
        if g_eng == "dve":
            S.add("dve", lambda e: e.tensor_tensor(z[:, 0:640], z[:, 0:640], gb[:, 0, 0:640], ALU.mult), reads=[z_b, gb_b], writes=[z_b])
            S.add("pool", lambda e: e.tensor_tensor(z[:, 640:1024], z[:, 640:1024], gb[:, 0, 640:1024], ALU.mult), reads=[z_b, gb_b], writes=[z_b])
        else:
            S.add(g_eng, lambda e: e.tensor_tensor(z[:], z[:], gb[:, 0, :], ALU.mult), reads=[z_b, gb_b], writes=[z_b])
        S.add(b_eng, lambda e: e.tensor_tensor(z[:], z[:], gb[:, 1, :], ALU.add), reads=[z_b, gb_b], writes=[z_b])
        op = S.add("pool", lambda e: e.dma_start(out=dst_ap, in_=z[:]), reads=[z_b], writes=[dst_buf], dma=z_b)
        if is_out:
            out_stores.append(op)

    def ln_epilogue_b(z, z_b, st, st_b, dst_ap, dst_buf, is_out):
        ln_epilogue_b1(z, z_b, st, st_b)
        ln_epilogue_b2(z, z_b, dst_ap, dst_buf, is_out)

    def load_gb(l, i):
        S.add("sp", lambda e: [e.dma_start(out=gb[:, 0, :], in_=lng_d[l * 3 + i:l * 3 + i + 1, :].broadcast_to([128, D])),
                               e.dma_start(out=gb[:, 1, :], in_=lnb_d[l * 3 + i:l * 3 + i + 1, :].broadcast_to([128, D]))],
              reads=[w_b], writes=[gb_b], dma=gb_b, ndma=2)

    def xt_prologue(src, src_bufs, tt, xbf, xbf_b, xT, xT_b, tp_banks):
        S.add("pool", lambda e: e.dma_start(out=xbf[:], in_=src[tt * 512:(tt + 1) * 512, :].rearrange("(j p) d -> p j d", p=128)),
              reads=list(src_bufs[tt]), writes=[xbf_b], dma=xbf_b)

    def xt_transposes(xbf, xbf_b, xT, xT_b, tp_banks, js=(0, 1, 2, 3)):
        for j in js:
            bk = tp_banks[j % len(tp_banks)]
            for k in range(8):
                S.add("pe", lambda e, j=j, k=k, bk=bk: e.transpose(PSB[:, bk, k * 128:(k + 1) * 128],
                                                                 xbf[:, j, k * 128:(k + 1) * 128], ident[:]),
                      reads=[xbf_b, ident_b], writes=[psb[bk]])
            eng = "dve" if j % 2 == 0 else "act"
            if eng == "dve":
                S.add("dve", lambda e, j=j, bk=bk: e.tensor_copy(xT[:, :, j * 128:(j + 1) * 128],
                                                                 PSB[:, bk, :].rearrange("p (k t) -> p k t", k=8)),
                      reads=[psb[bk]], writes=[xT_b])
            else:
                S.add("act", lambda e, j=j, bk=bk: e.copy(xT[:, :, j * 128:(j + 1) * 128],
                                                          PSB[:, bk, :].rearrange("p (k t) -> p k t", k=8)),
                      reads=[psb[bk]], writes=[xT_b])

    def phase_ffn(l, which, src, src_bufs, dst, dst_bufs, is_out):
        S.barrier()
        wi_d = (f1i_d if which == 1 else f2i_d)
        wo_d = (f1o_d if which == 1 else f2o_d)
        L = Layout(nc, COMMON_END)
        wi, _ = L.t("wi", [128, 8, 2 * DFF], BF16)
        wi_b = [Buf(f"wi{g}") for g in range(11)]
        wo, wo_b = L.t("wo", [128, NF, D], BF16)
        xbf, xbf_b = L.t("xbf", [128, 4, D], BF16)
        xT, xT_b = L.t("xT", [128, 8, 512], BF16)
        gT, gT_b = L.t("gT", [128, NF, 512], BF16)
        sil = [L.t(f"sil{i}", [128, 512], F32) for i in range(2)]
        xres = [L.t(f"xres{i}", [128, D], F32) for i in range(2)]
        zz = [L.t(f"z{i}", [128, D], F32) for i in range(2)]
        wsrc = wi_d[l].rearrange("(k p) n -> p k n", p=128)
        for g in range(11):
            S.add("pool", lambda e, g=g: [
                e.dma_start(out=wi[:, :, g * 256:(g + 1) * 256], in_=wsrc[:, :, g * 256:(g + 1) * 256]),
                e.dma_start(out=wi[:, :, DFF + g * 256:DFF + (g + 1) * 256], in_=wsrc[:, :, DFF + g * 256:DFF + (g + 1) * 256])],
                reads=[w_b], writes=[wi_b[g]], dma=wi_b[g], ndma=2)
            if g == 0:
                xt_prologue(src, src_bufs, 0, xbf, xbf_b, xT, xT_b, None)
        wosrc = wo_d[l].rearrange("(f p) n -> p f n", p=128)
        S.add("pool", lambda e: [e.dma_start(out=wo[:, 0:11, :], in_=wosrc[:, 0:11, :]),
                                 e.dma_start(out=wo[:, 11:22, :], in_=wosrc[:, 11:22, :])],
              reads=[w_b], writes=[wo_b], dma=wo_b, ndma=2)
        load_gb(l, 0 if which == 1 else 2)
        xt_transposes(xbf, xbf_b, xT, xT_b, (0, 2))
        pend = [None]
        for tt in range(NTT):
            if tt + 1 < NTT:
                xt_prologue(src, src_bufs, tt + 1, xbf, xbf_b, xT, xT_b, None)
            for f in range(NF):
                hb = f % 2
                bg, bu = 2 * hb, 2 * hb + 1
                for k in range(8):
                    S.add("pe", lambda e, f=f, k=k, bg=bg: e.matmul(PS[:, bg, :], wi[:, k, f * 128:(f + 1) * 128], xT[:, k, :],
                                                                    start=(k == 0), stop=(k == 7)),
                          reads=[wi_b[f // 2], xT_b], writes=[psb[bg]])
                for k in range(8):
                    S.add("pe", lambda e, f=f, k=k, bu=bu: e.matmul(PS[:, bu, :], wi[:, k, DFF + f * 128:DFF + (f + 1) * 128], xT[:, k, :],
                                                                    start=(k == 0), stop=(k == 7)),
                          reads=[wi_b[f // 2], xT_b], writes=[psb[bu]])
                sl, sl_b = sil[hb]
                S.add("act", lambda e, sl=sl, bg=bg: e.activation(sl[:], PS[:, bg, :], AF.Silu), reads=[psb[bg]], writes=[sl_b])
                S.add("dve", lambda e, sl=sl, bu=bu, f=f: e.tensor_tensor(gT[:, f, :], PS[:, bu, :], sl[:], ALU.mult),
                      reads=[psb[bu], sl_b], writes=[gT_b])
            for j in range(4):
                yb = j % 2
                b0 = 4 + 2 * yb
                xr, xr_b = xres[yb]
                r0 = tt * 512 + j * 128
                S.add("sp", lambda e, xr=xr, r0=r0: e.dma_start(out=xr[:], in_=src[r0:r0 + 128, :]),
                      reads=[src_bufs[tt][j]], writes=[xr_b], dma=xr_b)
                for n in range(2):
                    for f in range(NF):
                        S.add("pe", lambda e, j=j, n=n, f=f, b0=b0: e.matmul(PS[:, b0 + n, :], gT[:, f, j * 128:(j + 1) * 128],
                                                                             wo[:, f, n * 512:(n + 1) * 512],
                                                                             start=(f == 0), stop=(f == NF - 1)),
                              reads=[gT_b, wo_b], writes=[psb[b0 + n]])
                if tt + 1 < NTT and j < 2:
                    xt_transposes(xbf, xbf_b, xT, xT_b, (0, 2), js=(2 * j, 2 * j + 1))
                z, z_b = zz[yb]
                st, st_b = stt[yb]
                ln_epilogue_a(PS[:, b0:b0 + 2, :], (psb[b0], psb[b0 + 1]), 0.5, xr, xr_b, z, z_b, st, st_b)
                ln_epilogue_b(z, z_b, st, st_b, dst[r0:r0 + 128, :], dst_bufs[tt][j], is_out)

    def phase_proj(l, src, src_bufs):
        S.barrier()
        L = Layout(nc, COMMON_END)
        wqk, wqk_b = L.t("wqk", [128, 8, 1792], BF16)
        wrot, wrot_b = L.t("wrot", [128, 8, 1792], BF16)
        wv, wv_b = L.t("wv", [128, 8, 640], BF16)
        xbf, xbf_b = L.t("xbf", [128, 4, D], BF16)
        xT, xT_b = L.t("xT", [128, 8, 512], BF16)
        cs = [L.t(f"cs{i}", [128, 2, 512], F32) for i in range(2)]
        qko = [L.t(f"qko{i}", [128, 14, 512], BF16) for i in range(2)]
        vo = [L.t(f"vo{i}", [128, 4, 640], BF16) for i in range(2)]
        t1 = [L.t(f"t1_{i}", [128, 512], F32) for i in range(2)]
        t2 = [L.t(f"t2_{i}", [128, 512], F32) for i in range(2)]
        wsrc = w_in_d[l].rearrange("(k p) n -> p k n", p=128)
        xt_prologue(src, src_bufs, 0, xbf, xbf_b, xT, xT_b, None)
        S.add("pool", lambda e: [
            e.dma_start(out=wqk[:, :, 0:1024], in_=wsrc[:, :, 0:1024]),
            e.dma_start(out=wqk[:, :, 1024:1664], in_=wsrc[:, :, 1536:2176]),
            e.dma_start(out=wqk[:, :, 1664:1728], in_=wsrc[:, :, 2112:2176]),
            e.dma_start(out=wqk[:, :, 1728:1792], in_=wsrc[:, :, 2048:2112])],
            reads=[w_b], writes=[wqk_b], dma=wqk_b, ndma=4)
        S.add("pool", lambda e: [
            e.dma_start(out=wv[:, :, 0:512], in_=wsrc[:, :, 1024:1536]),
            e.dma_start(out=wv[:, :, 512:640], in_=wsrc[:, :, 2176:2304])],
            reads=[w_b], writes=[wv_b], dma=wv_b, ndma=2)
        for k in range(8):
            a = wqk[:, k, :].rearrange("p (h w j) -> p h w j", w=2, j=32)
            b = wrot[:, k, :].rearrange("p (h w j) -> p h w j", w=2, j=32)
            S.add("act", lambda e, a=a, b=b: e.mul(b[:, :, 0, :], a[:, :, 1, :], -1.0), reads=[wqk_b], writes=[wrot_b])
            S.add("dve", lambda e, a=a, b=b: e.tensor_copy(b[:, :, 1, :], a[:, :, 0, :]), reads=[wqk_b], writes=[wrot_b])
        for tt in range(NTT):
            s = tt // 8
            p0 = (tt % 8) * 512
            pb = tt % 2
            c_, c_b = cs[pb]
            S.add("sp", lambda e, c_=c_, s=s, p0=p0: e.dma_start(out=c_[:], in_=ROPE[s].rearrange("c p t -> p c t")[:, :, p0:p0 + 512]),
                  reads=[ROPE_b[s]], writes=[c_b], dma=c_b)
            xt_transposes(xbf, xbf_b, xT, xT_b, (7,))
            if tt + 1 < NTT:
                xt_prologue(src, src_bufs, tt + 1, xbf, xbf_b, xT, xT_b, None)
            q, q_b = qko[pb]
            for c in range(14):
                ab = c % 2
                bm, br = ab, 2 + ab
                for k in range(8):
                    S.add("pe", lambda e, c=c, k=k, bm=bm: e.matmul(PS[:, bm, :], wqk[:, k, c * 128:(c + 1) * 128], xT[:, k, :],
                                                                    start=(k == 0), stop=(k == 7)),
                          reads=[wqk_b, xT_b], writes=[psb[bm]])
                for k in range(8):
                    S.add("pe", lambda e, c=c, k=k, br=br: e.matmul(PS[:, br, :], wrot[:, k, c * 128:(c + 1) * 128], xT[:, k, :],
                                                                    start=(k == 0), stop=(k == 7)),
                          reads=[wrot_b, xT_b], writes=[psb[br]])
                a1, a1_b = t1[ab]
                a2, a2_b = t2[ab]
                S.add("dve", lambda e, a1=a1, bm=bm, c_=c_: e.tensor_tensor(a1[:], PS[:, bm, :], c_[:, 0, :], ALU.mult),
                      reads=[psb[bm], c_b], writes=[a1_b])
                S.add("dve", lambda e, a2=a2, br=br, c_=c_: e.tensor_tensor(a2[:], PS[:, br, :], c_[:, 1, :], ALU.mult),
                      reads=[psb[br], c_b], writes=[a2_b])
                S.add("pool", lambda e, a1=a1, a2=a2, q=q, c=c: e.tensor_tensor(q[:, c, :], a1[:], a2[:], ALU.add),
                      reads=[a1_b, a2_b], writes=[q_b])
            S.add("sp", lambda e, q=q, tt=tt: e.dma_start(out=QKT.rearrange("c p t -> p c t")[:, :, tt * 512:(tt + 1) * 512], in_=q[:]),
                  reads=[q_b], writes=[QKT_b[c][tt] for c in range(14)], dma=q_b)
            v, v_b = vo[pb]
            for j in range(4):
                for k in range(8):
                    S.add("pe", lambda e, j=j, k=k: e.matmul(PS[:, 4, :], xT[:, k, j * 128:(j + 1) * 128], wv[:, k, 0:512],
                                                             start=(k == 0), stop=(k == 7)),
                          reads=[wv_b, xT_b], writes=[psb[4]])
                for k in range(8):
                    S.add("pe", lambda e, j=j, k=k: e.matmul(PS[:, 5, 0:128], xT[:, k, j * 128:(j + 1) * 128], wv[:, k, 512:640],
                                                             start=(k == 0), stop=(k == 7)),
                          reads=[wv_b, xT_b], writes=[psb[5]])
                S.add("act", lambda e, v=v, j=j: e.copy(v[:, j, 0:512], PS[:, 4, :]), reads=[psb[4]], writes=[v_b])
                S.add("act", lambda e, v=v, j=j: e.copy(v[:, j, 512:640], PS[:, 5, 0:128]), reads=[psb[5]], writes=[v_b])
            S.add("sp", lambda e, v=v, tt=tt: e.dma_start(out=VS[tt * 512:(tt + 1) * 512, :].rearrange("(j p) d -> p j d", p=128), in_=v[:]),
                  reads=[v_b], writes=[VS_b[tt]], dma=v_b)

    def phase_attn(l, src, src_bufs, dst, dst_bufs, is_out, do_mix=True):
        S.barrier()
        L = Layout(nc, COMMON_END)
        wo, wo_b = L.t("wmo", [128, 8, D], BF16)
        qth = [L.t(f"qth{i}", [128, SEQ], BF16) for i in range(2)]
        kth = [L.t(f"kth{i}", [128, SEQ], BF16) for i in range(2)]
        vau = [L.t(f"vau{i}", [128, 32, 130], BF16) for i in range(2)]
        pt = [L.t(f"pt{i}", [128, 1024], BF16) for i in range(4)]

        ostg, ostg_b = L.t("ostg", [128, 9, 130], F32)
        rall, rall_b = L.t("rall", [128, 24], F32)
        aall, aall_b = L.t("aall", [128, 4, 128], F32)
        ball, ball_b = L.t("ball", [128, 4, 128], F32)
        oall, oall_b = L.t("oall", [128, 4, 128], F32)
        yall, yall_b = L.t("yall", [128, 4, 128], BF16)
        yts = [L.t(f"yts{i}", [128, 512], BF16) for i in range(2)]
        sq, sq_b = L.t("sq", [128, 4, SEQ], BF16)
        skk, skk_b = L.t("skk", [128, 2, SEQ], BF16)
        vsw, vsw_b = L.t("vsw", [128, 32, 2, 66], BF16)
        mk, mk_b = L.t("mk", [128, 768], BF16)
        pts = [L.t(f"pts{i}", [128, 768], BF16) for i in range(4)]
        den = [L.t(f"den{i}", [128, 8], F32) for i in range(2)]
        ysw = [L.t(f"ysw{i}", [128, 256], BF16) for i in range(2)]
        ytsw = [L.t(f"ytsw{i}", [128, 4, 512], BF16) for i in range(2)]
        ytt, ytt_b = L.t("ytt", [128, 8, 512], BF16)
        xres = [L.t(f"xres{i}", [128, D], F32) for i in range(2)]
        zz = [L.t(f"z{i}", [128, D], F32) for i in range(5)]
        stl = [L.t(f"stl{i}", [128, 32], F32) for i in range(5)]

        S.add("pool", lambda e: e.dma_start(out=wo[:], in_=w_out_d[l].rearrange("(k p) n -> p k n", p=128)),
              reads=[w_b], writes=[wo_b], dma=wo_b)
        S.add("pool", lambda e: e.dma_start(out=mk[:], in_=mask_d), reads=[w_b], writes=[mk_b], dma=mk_b)
        for i in range(2):
            S.add("pool", lambda e, i=i: e.memset(vau[i][0][:, :, 128:130], 1.0), writes=[vau[i][1]])
        S.add("pool", lambda e: e.memset(vsw[:, :, :, 64:66], 1.0), writes=[vsw_b])
        load_gb(l, 1)

        OSLOT = [(4 + i // 3, (i % 3) * 130) for i in range(8)]

        def do_seq(s):
            T0 = s * SEQ
            tts = range(s * 8, s * 8 + 8)
            steps = []
            for h in range(4):
                for qi in range(8):
                    for kc in range(32):
                        steps.append((h, qi, kc))
            deferred = {}

            def load_head(h):
                hb = h % 2
                q, q_b = qth[hb]
                k_, k_b = kth[hb]
                v, v_b = vau[hb]
                S.add("sp", lambda e: e.dma_start(out=q[:], in_=QKT[h][:, T0:T0 + SEQ]),
                      reads=[QKT_b[h][t] for t in tts], writes=[q_b], dma=q_b)
                S.add("sp", lambda e: e.dma_start(out=k_[:], in_=QKT[4 + h][:, T0:T0 + SEQ]),
                      reads=[QKT_b[4 + h][t] for t in tts], writes=[k_b], dma=k_b)
                S.add("sp", lambda e: [e.dma_start(out=v[:, 8 * i:8 * i + 8, 0:128],
                                                   in_=VS[T0 + i * 1024:T0 + (i + 1) * 1024, h * 128:(h + 1) * 128].rearrange("(c p) d -> p c d", p=128))
                                       for i in range(4)],
                      reads=[VS_b[t] for t in tts], writes=[v_b], dma=v_b, ndma=4)

            def emit_qk(h, qi, kc, idx):
                hb = h % 2
                q, q_b = qth[hb]
                k_, k_b = kth[hb]
                sb = idx % 2
                p_, p_b = pt[idx % 4]
                S.add("pe", lambda e: e.matmul(PS[:, 2 * sb, :], k_[0:64, kc * 128:(kc + 1) * 128], q[0:64, qi * 512:(qi + 1) * 512],
                                               start=True, stop=True),
                      reads=[q_b, k_b], writes=[psb[2 * sb]])
                S.add("pe", lambda e: e.matmul(PS[:, 2 * sb + 1, :], k_[64:128, kc * 128:(kc + 1) * 128], q[64:128, qi * 512:(qi + 1) * 512],
                                               start=True, stop=True),
                      reads=[q_b, k_b], writes=[psb[2 * sb + 1]])
                S.add("act", lambda e: e.activation(p_[:].rearrange("p (a b) -> p a b", a=2), PS[:, 2 * sb:2 * sb + 2, :], AF.Exp, scale=SCALE),
                      reads=[psb[2 * sb], psb[2 * sb + 1]], writes=[p_b])

            def emit_pv(h, qi, kc, idx):
                hb = h % 2
                v, v_b = vau[hb]
                p_, p_b = pt[idx % 4]
                for i in range(8):
                    c, qs = i // 4, i % 4
                    bk, col = OSLOT[i]
                    first = (kc == 0) and (i % 3 == 0)
                    S.add("pe", lambda e, c=c, qs=qs, bk=bk, col=col, first=first: e.matmul(
                        PS[:, bk, col:col + 129], p_[:, c * 512 + qs * 128:c * 512 + (qs + 1) * 128], v[:, kc, 0:129],
                        start=first, stop=(kc == 31), skip_group_check=True),
                        reads=[p_b, v_b], writes=[psb[bk]])
                if FILLER:
                    q, q_b = qth[hb]
                    S.add("pe", lambda e: e.matmul(PS[:, 7, 256:384], ident[:], q[:, qi * 512:qi * 512 + 128], start=True, stop=True,
                                                   skip_group_check=True),
                          reads=[q_b, ident_b], writes=[psb[7]])

            def epi1(h, qi):
                for b in range(3):
                    S.add("dve", lambda e, b=b: e.tensor_copy(ostg[:, 3 * b:3 * b + 3, :].rearrange("p a b -> p (a b)"), PS[:, 4 + b, 0:390]),
                          reads=[psb[4 + b]], writes=[ostg_b])
                S.add("dve", lambda e: e.reciprocal(rall[:, 0:8], ostg[:, 0:8, 128]), reads=[ostg_b], writes=[rall_b])
                S.add("dve", lambda e: e.tensor_scalar(rall[:, 8:12], rall[:, 4:8], neglam[:, l:l + 1], None, ALU.mult),
                      reads=[rall_b, neglam_b], writes=[rall_b])
                S.add("dve", lambda e: e.tensor_tensor(aall[:], ostg[:, 0:4, 0:128], bcast_last(rall[:, 0:4], 128), ALU.mult),
                      reads=[ostg_b, rall_b], writes=[aall_b])
                S.add("dve", lambda e: e.tensor_tensor(ball[:], ostg[:, 4:8, 0:128], bcast_last(rall[:, 8:12], 128), ALU.mult),
                      reads=[ostg_b, rall_b], writes=[ball_b])
                S.add("dve", lambda e: e.tensor_tensor(oall[:], aall[:], ball[:], ALU.add), reads=[aall_b, ball_b], writes=[oall_b])

            def epi2(h, qi):
                yt_, yt_b = yts[qi % 2]
                S.add("dve", lambda e: e.tensor_tensor(aall[:], oall[:], oall[:], ALU.mult), reads=[oall_b], writes=[aall_b])
                S.add("dve", lambda e: e.reduce_sum(rall[:, 12:16], aall[:], axis=AX.X), reads=[aall_b], writes=[rall_b])
                S.add("dve", lambda e: e.tensor_scalar(rall[:, 16:20], rall[:, 12:16], 1.0 / 128.0, RMS_EPS, ALU.mult, ALU.add),
                      reads=[rall_b], writes=[rall_b])
                S.add("pool", lambda e: e.tensor_tensor(rall[:, 20:24], rall[:, 16:20], bcast_cols(negh[:, 0:1], 4), ALU.pow),
                      reads=[rall_b, negh_b], writes=[rall_b])
                S.add("dve", lambda e: e.tensor_tensor(ball[:], oall[:], bcast_last(rall[:, 20:24], 128), ALU.mult),
                      reads=[oall_b, rall_b], writes=[ball_b])
                gs = gsub[:, l, :]
                gsb = bass.AP(gs.tensor, gs.offset, [list(gs.ap[0]), [0, 4], list(gs.ap[-1])])
                S.add("dve", lambda e: e.tensor_tensor(yall[:], ball[:], gsb, ALU.mult), reads=[ball_b, gsub_b], writes=[yall_b])
                for qs in range(4):
                    S.add("pe", lambda e, qs=qs: e.transpose(PSB[:, 7, qs * 128:(qs + 1) * 128], yall[:, qs, :], ident[:]),
                          reads=[yall_b, ident_b], writes=[psb[7]])
                S.add("dve", lambda e: e.tensor_copy(yt_[:], PSB[:, 7, 0:512]), reads=[psb[7]], writes=[yt_b])
                tq = s * 8 + qi
                S.add("pool", lambda e: e.dma_start(out=YT[h][:, T0 + qi * 512:T0 + (qi + 1) * 512], in_=yt_[:]),
                      reads=[yt_b], writes=[YT_b[h][tq]], dma=yt_b)

            n = len(steps)
            load_head(0)
            LAG = 2
            for i in range(n + LAG):
                if i < n:
                    h, qi, kc = steps[i]
                    emit_qk(h, qi, kc, i)
                    if qi == 1 and kc == 0 and h + 1 < 4:
                        load_head(h + 1)
                if i >= LAG:
                    h, qi, kc = steps[i - LAG]
                    emit_pv(h, qi, kc, i - LAG)
                    if kc == 31:
                        epi1(h, qi)
                        deferred.setdefault(i + 12, []).append((h, qi))
                for (hh, qq) in deferred.pop(i, []):
                    epi2(hh, qq)
            for key in sorted(deferred):
                for (hh, qq) in deferred[key]:
                    epi2(hh, qq)

            S.add("sp", lambda e: [e.dma_start(out=sq[:, c, :], in_=QKT[8 + c][:, T0:T0 + SEQ]) for c in range(4)],
                  reads=[QKT_b[8 + c][t] for c in range(4) for t in tts], writes=[sq_b], dma=sq_b, ndma=4)
            S.add("sp", lambda e: [e.dma_start(out=skk[:, c, :], in_=QKT[12 + c][:, T0:T0 + SEQ]) for c in range(2)],
                  reads=[QKT_b[12 + c][t] for c in range(2) for t in tts], writes=[skk_b], dma=skk_b, ndma=2)
            S.add("sp", lambda e: [e.dma_start(out=vsw[:, 8 * i:8 * i + 8, g, 0:64],
                                               in_=VS[T0 + i * 1024:T0 + (i + 1) * 1024, 512 + g * 64:512 + (g + 1) * 64].rearrange("(c p) d -> p c d", p=128))
                                   for g in range(2) for i in range(4)],
                  reads=[VS_b[t] for t in tts], writes=[vsw_b], dma=vsw_b, ndma=8)
            def swa_a(blk, g, it):
                dlo = -1 if blk > 0 else 0
                dhi = 1 if blk < 31 else 0
                bufi = it % 2
                plo, plo_b = pts[2 * bufi]
                phi, phi_b = pts[2 * bufi + 1]
                klo_c = 0 if g == 0 else 1
                khi_c = 1 if g == 0 else 0
                for hl in range(2):
                    for dl_ in range(dlo, dhi + 1):
                        di = dl_ + 1
                        bi = hl * 3 + di
                        kb = blk + dl_
                        S.add("pe", lambda e, hl=hl, bi=bi, kb=kb: e.matmul(
                            PS[:, bi // 4, (bi % 4) * 128:(bi % 4 + 1) * 128],
                            skk[0:64, klo_c, kb * 128:(kb + 1) * 128], sq[0:64, 2 * g + hl, blk * 128:(blk + 1) * 128],
                            start=True, stop=True),
                            reads=[skk_b, sq_b], writes=[psb[bi // 4]])
                        S.add("pe", lambda e, hl=hl, bi=bi, kb=kb: e.matmul(
                            PS[:, 2 + bi // 4, (bi % 4) * 128:(bi % 4 + 1) * 128],
                            skk[64:128, khi_c, kb * 128:(kb + 1) * 128], sq[64:128, 2 * g + hl, blk * 128:(blk + 1) * 128],
                            start=True, stop=True),
                            reads=[skk_b, sq_b], writes=[psb[2 + bi // 4]])
                S.add("act", lambda e: e.activation(plo[:], PS[:, 0:2, :].rearrange("p a b -> p (a b)")[:, 0:768], AF.Exp, scale=SCALE),
                      reads=[psb[0], psb[1]], writes=[plo_b])
                S.add("act", lambda e: e.activation(phi[:], PS[:, 2:4, :].rearrange("p a b -> p (a b)")[:, 0:768], AF.Exp, scale=SCALE),
                      reads=[psb[2], psb[3]], writes=[phi_b])
                S.add("dve", lambda e: e.tensor_tensor(plo[:], plo[:], mk[:], ALU.mult), reads=[plo_b, mk_b], writes=[plo_b])
                S.add("dve", lambda e: e.tensor_tensor(phi[:, 0:256], phi[:, 0:256], mk[:, 0:256], ALU.mult), reads=[phi_b, mk_b], writes=[phi_b])
                S.add("pool", lambda e: e.tensor_tensor(phi[:, 256:768], phi[:, 256:768], mk[:, 256:768], ALU.mult), reads=[phi_b, mk_b], writes=[phi_b])

            def swa_b(blk, g, it):
                dlo = -1 if blk > 0 else 0
                dhi = 1 if blk < 31 else 0
                bufi = it % 2
                plo, plo_b = pts[2 * bufi]
                phi, phi_b = pts[2 * bufi + 1]
                ob_ = 4 + 2 * (it % 2)
                firstmm = True
                for hd in range(4):
                    hl, half = hd // 2, hd % 2
                    p_, p_b = (plo, plo_b) if half == 0 else (phi, phi_b)
                    for dl_ in range(dlo, dhi + 1):
                        di = dl_ + 1
                        bi = hl * 3 + di
                        kb = blk + dl_
                        S.add("pe", lambda e, hd=hd, bi=bi, kb=kb, p_=p_, firstmm=firstmm, last=(dl_ == dhi and hd == 3): e.matmul(
                            PS[:, ob_, hd * 66:hd * 66 + 65], p_[:, bi * 128:(bi + 1) * 128], vsw[:, kb, g, 0:65],
                            start=firstmm, stop=last, skip_group_check=True),
                            reads=[p_b, vsw_b], writes=[psb[ob_]])
                        firstmm = False
                dn, dn_b = den[bufi]
                yw, yw_b = ysw[bufi]
                o3 = PS[:, ob_, 0:264].rearrange("p (h d) -> p h d", d=66)
                S.add("dve", lambda e: e.tensor_tensor(dn[:, 0:4], o3[:, :, 64], esink[:, l * 8 + 4 * g:l * 8 + 4 * g + 4], ALU.add),
                      reads=[psb[ob_], esink_b], writes=[dn_b])
                S.add("dve", lambda e: e.reciprocal(dn[:, 4:8], dn[:, 0:4]), reads=[dn_b], writes=[dn_b])
                S.add("dve", lambda e: e.tensor_tensor(yw[:].rearrange("p (h d) -> p h d", d=64), o3[:, :, 0:64],
                                                       bcast_last(dn[:, 4:8], 64), ALU.mult),
                      reads=[psb[ob_], dn_b], writes=[yw_b])

            def swa_c(blk, g, it):
                bufi = it % 2
                yw, yw_b = ysw[bufi]
                ys_, ys_b = ytsw[(blk // 4) % 2]
                for cc in range(2):
                    S.add("pe", lambda e, cc=cc: e.transpose(PSB[:, 5, cc * 128:(cc + 1) * 128], yw[:, cc * 128:(cc + 1) * 128], ident[:]),
                          reads=[yw_b, ident_b], writes=[psb[5]])
                S.add("dve", lambda e: e.tensor_copy(ys_[:, 2 * g:2 * g + 2, (blk % 4) * 128:(blk % 4 + 1) * 128],
                                                     PSB[:, 5, 0:256].rearrange("p (c t) -> p c t", c=2)),
                      reads=[psb[5]], writes=[ys_b])
                if blk % 4 == 3 and g == 1:
                    tq = s * 8 + blk // 4
                    S.add("pool", lambda e: e.dma_start(
                        out=YT[4:8].rearrange("c p t -> p c t")[:, :, tq * 512:(tq + 1) * 512], in_=ys_[:]),
                        reads=[ys_b], writes=[YT_b[4 + c][tq] for c in range(4)], dma=ys_b)

            its = [(blk, g) for blk in range(32) for g in range(2)]
            for it in range(len(its) + 2):
                if it < len(its):
                    swa_a(its[it][0], its[it][1], it)
                if 1 <= it <= len(its):
                    swa_b(its[it - 1][0], its[it - 1][1], it - 1)
                if it >= 2:
                    swa_c(its[it - 2][0], its[it - 2][1], it - 2)

            if not do_mix:
                return
            pend = [None]
            pend2 = [None]
            for tq in tts:
                S.add("sp", lambda e, tq=tq: e.dma_start(out=ytt[:], in_=YT.rearrange("c p t -> p c t")[:, :, tq * 512:(tq + 1) * 512]),
                      reads=[YT_b[c][tq] for c in range(8)], writes=[ytt_b], dma=ytt_b)
                for j in range(4):
                    yb = j % 2
                    b0 = 4 + 2 * yb
                    xr, xr_b = xres[yb]
                    r0 = tq * 512 + j * 128
                    S.add("sp", lambda e, xr=xr, r0=r0: e.dma_start(out=xr[:], in_=src[r0:r0 + 128, :]),
                          reads=[src_bufs[tq][j]], writes=[xr_b], dma=xr_b)
                    for n_ in range(2):
                        for c in range(8):
                            S.add("pe", lambda e, j=j, n_=n_, c=c, b0=b0: e.matmul(PS[:, b0 + n_, :], ytt[:, c, j * 128:(j + 1) * 128],
                                                                                 wo[:, c, n_ * 512:(n_ + 1) * 512],
                                                                                 start=(c == 0), stop=(c == 7)),
                                  reads=[ytt_b, wo_b], writes=[psb[b0 + n_]])
                    zi = (tq * 4 + j) % 5
                    z, z_b = zz[zi]
                    st, st_b = stl[zi]
                    ln_epilogue_a(PS[:, b0:b0 + 2, :], (psb[b0], psb[b0 + 1]), 1.0, xr, xr_b, z, z_b, st, st_b)
                    if pend[0] is not None:
                        p_ = pend[0]
                        ln_epilogue_b1(p_[0], p_[1], p_[2], p_[3], zn_eng="act")
                    if pend2[0] is not None:
                        p_ = pend2[0]
                        ln_epilogue_b2(p_[0], p_[1], p_[4], p_[5], p_[6], g_eng="dve", b_eng="pool")
                    pend2[0] = pend[0]
                    pend[0] = (z, z_b, st, st_b, dst[r0:r0 + 128, :], dst_bufs[tq][j], is_out)
            if pend[0] is not None:
                p_ = pend[0]
                ln_epilogue_b1(p_[0], p_[1], p_[2], p_[3], zn_eng="act")
            if pend2[0] is not None:
                p_ = pend2[0]
                ln_epilogue_b2(p_[0], p_[1], p_[4], p_[5], p_[6], g_eng="dve", b_eng="pool")
            if pend[0] is not None:
                p_ = pend[0]
                ln_epilogue_b2(p_[0], p_[1], p_[4], p_[5], p_[6], g_eng="dve", b_eng="pool")
            pend[0] = None
            pend2[0] = None

        for s_ in range(NSEQ):
            do_seq(s_)

    phase_rope(phase_scalars())
    done = False
    if isinstance(stop, list):
        cur, cur_b = x_in, XIN_b
        for i, ph in enumerate(stop):
            lastp = (i == len(stop) - 1)
            d_, d_b = (out_d, OUT_b) if lastp else (XS, XS_b)
            if ph == "ffn1":
                phase_ffn(0, 1, cur, cur_b, d_, d_b, lastp)
            elif ph == "ffn2":
                phase_ffn(0, 2, cur, cur_b, d_, d_b, lastp)
            elif ph == "proj":
                phase_proj(0, cur, cur_b)
                continue
            elif ph == "attn":
                phase_attn(0, cur, cur_b, d_, d_b, lastp)
            cur, cur_b = XS, XS_b
        depth = 0
    for l in range(0 if stop != "rope" else depth, depth):
        last_layer = (l == depth - 1)
        src, src_bufs = (x_in, XIN_b) if l == 0 else (XS, XS_b)
        fin = stop == (l, "ffn1")
        phase_ffn(l, 1, src, src_bufs, out_d if fin else XS, OUT_b if fin else XS_b, fin)
        if fin:
            break
        if stop == (l, "proj"):
            phase_proj(l, XS, XS_b)
            break
        phase_proj(l, XS, XS_b)
        if stop == (l, "attn"):
            phase_attn(l, XS, XS_b, XS, XS_b, False, do_mix=False)
            break
        fin = stop == (l, "mix")
        phase_attn(l, XS, XS_b, out_d if fin else XS, OUT_b if fin else XS_b, fin)
        if fin:
            break
        fin = last_layer or stop == (l, "ffn2")
        phase_ffn(l, 2, XS, XS_b, out_d if fin else XS, OUT_b if fin else XS_b, fin)
        if fin:
            break
    S.add("sp", None, extra=list(out_stores) + list(S.dmas_since))
    S.emit()
    S.close()
    return nc


def make_consts():
    ident = np.eye(128, dtype=np.float32)
    inv = np.power(np.float32(10000.0), -(np.arange(0, 64, 2, dtype=np.float32) / np.float32(64))).astype(np.float32)
    inv_tab = np.tile(inv, 4).reshape(128, 1).astype(np.float32)
    j = np.arange(128)[:, None]
    i = np.arange(128)[None, :]
    m = np.ones((128, 2, 3, 128), dtype=np.float32)
    m[:, :, 0, :] = (j >= i)[:, None, :]
    m[:, :, 2, :] = (j <= i)[:, None, :]
    return ident, inv_tab, np.ascontiguousarray(m.reshape(128, 768))


def make_in_maps(inputs, nseq, cores, wdepth=DEPTH):
    ident, inv_tab, mask = make_consts()
    x = np.ascontiguousarray(np.asarray(inputs["x"], dtype=np.float32))
    pos = np.ascontiguousarray(np.asarray(inputs["positions"], dtype=np.int32))
    shared = {
        "w_in": np.ascontiguousarray(np.asarray(inputs["w_in"], dtype=np.float32)[:wdepth]),
        "w_out": np.ascontiguousarray(np.asarray(inputs["w_out"], dtype=np.float32)[:wdepth]),
        "diff_lambda": np.asarray(inputs["diff_lambda"], dtype=np.float32).reshape(1, -1),
        "diff_subln_g": np.asarray(inputs["diff_subln_g"], dtype=np.float32).reshape(1, -1),
        "swa_sink": np.asarray(inputs["swa_sink"], dtype=np.float32).reshape(1, -1),
        "ffn1_w_in": np.ascontiguousarray(np.asarray(inputs["ffn1_w_in"], dtype=np.float32)[:wdepth]),
        "ffn1_w_out": np.ascontiguousarray(np.asarray(inputs["ffn1_w_out"], dtype=np.float32)[:wdepth]),
        "ffn2_w_in": np.ascontiguousarray(np.asarray(inputs["ffn2_w_in"], dtype=np.float32)[:wdepth]),
        "ffn2_w_out": np.ascontiguousarray(np.asarray(inputs["ffn2_w_out"], dtype=np.float32)[:wdepth]),
        "ln_g": np.asarray(inputs["ln_g"], dtype=np.float32).reshape(DEPTH * 3, D),
        "ln_b": np.asarray(inputs["ln_b"], dtype=np.float32).reshape(DEPTH * 3, D),
        "c_ident": ident, "c_inv": inv_tab, "c_mask": mask,
    }
    maps = []
    for c in cores:
        m = dict(shared)
        m["x"] = x[c * nseq:(c + 1) * nseq].reshape(nseq * SEQ, D)
        m["positions"] = pos[c * nseq:(c + 1) * nseq]
        maps.append(m)
    return maps


def kernel(**inputs):
    nseq = 2
    nc = build(NSEQ=nseq)
    maps = make_in_maps(inputs, nseq, list(range(NCORES)))
    res = run_bass_kernel_spmd(nc, maps, core_ids=list(range(NCORES)))
    outs = [np.asarray(r["out"], dtype=np.float32).reshape(nseq, SEQ, D) for r in res.results]
    return np.concatenate(outs, axis=0)
```

```python
from contextlib import ExitStack
import math
import numpy as np
import concourse.bass as bass
import concourse.mybir as mybir
from concourse.bass_utils import run_bass_kernel_spmd

F32 = mybir.dt.float32
BF16 = mybir.dt.bfloat16
I32 = mybir.dt.int32
U8 = mybir.dt.uint8
AF = mybir.ActivationFunctionType
ALU = mybir.AluOpType
AX = mybir.AxisListType

D = 1024
SEQ = 4096
DEPTH = 4
DFF = 2816
NF = DFF // 128
INC = 2304
ALPHA = (2.0 * DEPTH) ** 0.25
LN_EPS = 1e-5
RMS_EPS = 1e-5
SCALE = 0.125
NCORES = 8
SEM_WRAP = 30000
FILLER = False
ARENA_BASE = 16512
ARENA_SIZE = 212000


class Buf:
    __slots__ = ("name", "last_w", "readers", "dma_readers", "sem", "dma_total", "last_dma")

    def __init__(self, name):
        self.name = name
        self.last_w = None
        self.readers = {}
        self.dma_readers = []
        self.sem = {}
        self.dma_total = 0
        self.last_dma = {}


class Op:
    __slots__ = ("idx", "eng", "fn", "deps", "signaled", "ev", "is_dma", "dbuf", "ndma")


class PhysSem:
    __slots__ = ("handle", "total")

    def __init__(self):
        self.handle = None
        self.total = 0


class Sched:
    ENGS = ("pe", "act", "dve", "pool", "sp")

    def __init__(self, nc):
        self.nc = nc
        self.ops = []
        self.es = ExitStack()
        self.phys = []
        self.free_phys = {"hw": [], "sw": []}
        self.active_bufs = []
        self.last_on = {}
        self.dmas_since = []

    def buf(self, name):
        return Buf(name)

    def add(self, eng, fn, reads=(), writes=(), dma=None, ndma=1, extra=()):
        op = Op()
        op.idx = len(self.ops)
        op.eng = eng
        op.fn = fn
        op.is_dma = dma is not None
        op.dbuf = dma
        op.ndma = ndma
        op.signaled = op.is_dma
        op.ev = None
        hard = set()
        soft = set()
        for b in tuple(reads) + tuple(writes):
            if b.last_w is not None:
                hard.add(b.last_w)
        for b in writes:
            for r in b.readers.values():
                soft.add(r)
            for r in b.dma_readers:
                hard.add(r)
        for e in extra:
            if e is not None:
                hard.add(e)
        if op.is_dma:
            qc = "sw" if eng == "pool" else "hw"
            if dma.last_dma.get(qc) is not None:
                hard.add(dma.last_dma[qc])
            dma.last_dma[qc] = op
            if dma.sem.get(qc) is None:
                fp = self.free_phys[qc]
                while fp and fp[-1].total > 20000:
                    fp.pop()
                if fp:
                    dma.sem[qc] = fp.pop()
                else:
                    dma.sem[qc] = PhysSem()
                    self.phys.append(dma.sem[qc])
                self.active_bufs.append((dma, qc))
            ps_ = dma.sem[qc]
            ps_.total += 16 * ndma
            op.ev = (ps_, ps_.total)
            op.dbuf = ps_
            self.dmas_since.append(op)
        deps = set()
        for d in hard | soft:
            if d is op:
                continue
            if (not d.is_dma) and (not op.is_dma) and d.eng == eng:
                if eng == "pe":
                    continue
                if d not in hard:
                    continue
            deps.add(d)
        for d in deps:
            d.signaled = True
        op.deps = deps
        for b in writes:
            b.last_w = op
            b.readers = {}
            b.dma_readers = []
        ws = set(id(b) for b in writes)
        for b in reads:
            if id(b) in ws:
                continue
            if op.is_dma:
                b.dma_readers.append(op)
            else:
                b.readers[eng] = op
        self.ops.append(op)
        if fn is not None and not op.is_dma:
            self.last_on[eng] = op
        return op

    def barrier(self):
        deps = list(self.last_on.values()) + list(self.dmas_since)
        self.dmas_since = []
        for e in self.ENGS:
            self.add(e, None, extra=deps)
        for b, qc in self.active_bufs:
            self.free_phys[qc].append(b.sem[qc])
            b.sem[qc] = None
            b.last_dma[qc] = None
        self.active_bufs = []

    def emit(self):
        nc = self.nc
        es = self.es
        cnt = {e: 0 for e in self.ENGS}
        for op in self.ops:
            if not op.is_dma and op.signaled:
                cnt[op.eng] += 1
        import os
        if os.environ.get("KDBG"):
            print("signaled", cnt, "nphys", len(self.phys), "max_total", max(p.total for p in self.phys), "nops", len(self.ops), flush=True)
        esems = {}
        for e in self.ENGS:
            n = cnt[e] // SEM_WRAP + 1
            esems[e] = [es.enter_context(nc.semaphore(f"s_{e}_{i}")) for i in range(n)]
        for i, b in enumerate(self.phys):
            b.handle = es.enter_context(nc.semaphore(f"d{i}"))
        c = {e: 0 for e in self.ENGS}
        for op in self.ops:
            if not op.is_dma and op.signaled:
                k = c[op.eng]
                c[op.eng] += 1
                op.ev = (op.eng, k // SEM_WRAP, k % SEM_WRAP + 1)
        per = {e: [] for e in self.ENGS}
        for op in self.ops:
            per[op.eng].append(op)

        def run(eng_name, eng):
            waited_c = {}
            waited_d = {}
            for op in per[eng_name]:
                need_c = {}
                need_d = {}
                for d in op.deps:
                    if d.is_dma:
                        b, v = d.ev
                        if waited_d.get(id(b), 0) >= v:
                            continue
                        if need_d.get(id(b), (None, 0))[1] < v:
                            need_d[id(b)] = (b, v)
                    else:
                        if d.ev is None:
                            continue
                        pe_, k, v = d.ev
                        if waited_c.get(pe_, (-1, 0)) >= (k, v):
                            continue
                        if need_c.get(pe_, (-1, 0)) < (k, v):
                            need_c[pe_] = (k, v)
                for pe_, (k, v) in need_c.items():
                    eng.wait_ge(esems[pe_][k], v)
                    waited_c[pe_] = (k, v)
                for _, (b, v) in need_d.items():
                    eng.wait_ge(b.handle, v)
                    waited_d[id(b)] = v
                if op.fn is None:
                    continue
                r = op.fn(eng)
                if op.is_dma:
                    if not isinstance(r, (list, tuple)):
                        r = [r]
                    assert len(r) == op.ndma, (len(r), op.ndma)
                    for ins in r:
                        ins.then_inc(op.dbuf.handle, 16)
                elif op.signaled:
                    _, k, v = op.ev
                    r.then_inc(esems[op.eng][k], 1)

        with nc.Block() as block:
            @block.tensor
            def _(e):
                run("pe", e)

            @block.scalar
            def _(e):
                run("act", e)

            @block.vector
            def _(e):
                run("dve", e)

            @block.gpsimd
            def _(e):
                run("pool", e)

            @block.sync
            def _(e):
                run("sp", e)

    def close(self):
        self.es.close()


DT_SIZE = {F32: 4, BF16: 2, I32: 4, U8: 1}


class Layout:
    cnt = 0

    def __init__(self, nc, start):
        self.nc = nc
        self.off = start

    def t(self, name, shape, dtype):
        nbytes = DT_SIZE[dtype]
        for s in shape[1:]:
            nbytes *= s
        off = (self.off + 63) // 64 * 64
        self.off = off + nbytes
        assert self.off <= ARENA_SIZE, (name, self.off)
        Layout.cnt += 1
        h = self.nc.alloc_sbuf_tensor_at(f"{name}_{Layout.cnt}", list(shape), dtype, offset=ARENA_BASE + off)
        return h, Buf(name)


def bcast_cols(ap, n):
    return bass.AP(ap.tensor, ap.offset, [list(ap.ap[0]), [0, n]])


def bcast_last(ap, n):
    return bass.AP(ap.tensor, ap.offset, [list(x) for x in ap.ap] + [[0, n]])


def build(NSEQ=2, depth=DEPTH, stop=None, dbg=False, wdepth=DEPTH):
    nc = bass.Bass("TRN2", target_bir_lowering=False)
    NTOK = NSEQ * SEQ
    NTT = NTOK // 512
    ein = "ExternalInput"

    def dram(name, shape, dt, kind="Internal"):
        return nc.dram_tensor(name, list(shape), dt, kind=kind)

    x_in = dram("x", [NTOK, D], F32, ein).ap()
    pos_d = dram("positions", [NSEQ, SEQ], I32, ein).ap()
    w_in_d = dram("w_in", [wdepth, D, INC], F32, ein).ap()
    w_out_d = dram("w_out", [wdepth, D, D], F32, ein).ap()
    dlam_d = dram("diff_lambda", [1, DEPTH * 4 * 64], F32, ein).ap()
    subg_d = dram("diff_subln_g", [1, DEPTH * 128], F32, ein).ap()
    sink_d = dram("swa_sink", [1, DEPTH * 8], F32, ein).ap()
    f1i_d = dram("ffn1_w_in", [wdepth, D, 2 * DFF], F32, ein).ap()
    f1o_d = dram("ffn1_w_out", [wdepth, DFF, D], F32, ein).ap()
    f2i_d = dram("ffn2_w_in", [wdepth, D, 2 * DFF], F32, ein).ap()
    f2o_d = dram("ffn2_w_out", [wdepth, DFF, D], F32, ein).ap()
    lng_d = dram("ln_g", [DEPTH * 3, D], F32, ein).ap()
    lnb_d = dram("ln_b", [DEPTH * 3, D], F32, ein).ap()
    ident_d = dram("c_ident", [128, 128], F32, ein).ap()
    inv_d = dram("c_inv", [128, 1], F32, ein).ap()
    mask_d = dram("c_mask", [128, 2 * 3 * 128], F32, ein).ap()
    out_d = dram("out", [NTOK, D], F32, "ExternalOutput").ap()

    sk = "ExternalOutput" if dbg else "Internal"
    XS = dram("xs", [NTOK, D], F32).ap()
    QKT = dram("qkt", [14, 128, NTOK], BF16, sk).ap()
    VS = dram("vs", [NTOK, 640], BF16, sk).ap()
    YT = dram("yt", [8, 128, NTOK], BF16, sk).ap()
    ROPE = dram("rope", [NSEQ, 2, 128, SEQ], F32, sk).ap()

    S = Sched(nc)
    arena = nc.alloc_sbuf_tensor("arena", [128, ARENA_SIZE], U8)
    assert nc.lookup_mloc(arena).addr == ARENA_BASE, nc.lookup_mloc(arena).addr
    PS = nc.alloc_psum_tensor("ps", [128, 8, 512], F32)
    PSB = PS.bitcast(BF16)
    psb = [Buf(f"bank{i}") for i in range(8)]

    xin_b = Buf("xin")
    XIN_b = [[xin_b] * 4 for t in range(NTT)]
    XS_b = [[Buf(f"xs{t}_{j}") for j in range(4)] for t in range(NTT)]
    OUT_b = [[Buf(f"out{t}_{j}") for j in range(4)] for t in range(NTT)]
    QKT_b = [[Buf(f"qkt{c}_{t}") for t in range(NTT)] for c in range(14)]
    VS_b = [Buf(f"vs{t}") for t in range(NTT)]
    YT_b = [[Buf(f"yt{c}_{t}") for t in range(NTT)] for c in range(8)]
    ROPE_b = [Buf(f"rope{s}") for s in range(NSEQ)]
    w_b = Buf("weights_dram")
    out_b = Buf("out")
    out_stores = []

    L0 = Layout(nc, 0)
    ident, ident_b = L0.t("ident", [128, 128], BF16)
    negh, negh_b = L0.t("negh", [128, 1], F32)
    invt, invt_b = L0.t("invt", [128, 1], F32)
    gb, gb_b = L0.t("gb", [128, 2, D], F32)
    neglam, neglam_b = L0.t("neglam", [128, DEPTH], F32)
    gsub, gsub_b = L0.t("gsub", [128, DEPTH, 128], F32)
    esink, esink_b = L0.t("esink", [128, DEPTH * 8], F32)
    stt = [L0.t(f"st{i}", [128, 32], F32) for i in range(2)]
    COMMON_END = L0.off

    S.add("pool", lambda e: e.dma_start(out=ident[:], in_=ident_d), reads=[w_b], writes=[ident_b], dma=ident_b)
    S.add("sp", lambda e: e.dma_start(out=invt[:], in_=inv_d), reads=[w_b], writes=[invt_b], dma=invt_b)
    S.add("dve", lambda e: e.memset(negh[:], -0.5), writes=[negh_b])

    def phase_scalars():
        L = Layout(nc, COMMON_END)
        dl, dl_b = L.t("dl", [128, DEPTH * 256], F32)
        prod, prod_b = L.t("prod", [128, DEPTH * 128], F32)
        sums, sums_b = L.t("sums", [128, DEPTH * 2], F32)
        ee, ee_b = L.t("ee", [128, DEPTH * 2], F32)
        dd, dd_b = L.t("dd", [128, DEPTH], F32)
        sg, sg_b = L.t("sg", [128, DEPTH * 128], F32)
        sk_, sk_b = L.t("sk", [128, DEPTH * 8], F32)
        S.add("sp", lambda e: e.dma_start(out=dl[:], in_=dlam_d.broadcast_to([128, DEPTH * 256])),
              reads=[w_b], writes=[dl_b], dma=dl_b)
        S.add("sp", lambda e: e.dma_start(out=sg[:], in_=subg_d.broadcast_to([128, DEPTH * 128])),
              reads=[w_b], writes=[sg_b], dma=sg_b)
        S.add("sp", lambda e: e.dma_start(out=sk_[:], in_=sink_d.broadcast_to([128, DEPTH * 8])),
              reads=[w_b], writes=[sk_b], dma=sk_b)
        dl5 = dl[:].rearrange("p (a w d) -> p a w d", w=2, d=64)
        S.add("dve", lambda e: e.tensor_tensor(prod[:].rearrange("p (a d) -> p a d", d=64),
                                               dl5[:, :, 0, :], dl5[:, :, 1, :], ALU.mult),
              reads=[dl_b], writes=[prod_b])
        S.add("dve", lambda e: e.reduce_sum(sums[:], prod[:].rearrange("p (a d) -> p a d", d=64), axis=AX.X),
              reads=[prod_b], writes=[sums_b])
        S.add("act", lambda e: e.activation(ee[:], sums[:], AF.Exp), reads=[sums_b], writes=[ee_b])
        ee3 = ee[:].rearrange("p (l c) -> p l c", c=2)
        S.add("dve", lambda e: e.tensor_tensor(dd[:], ee3[:, :, 1], ee3[:, :, 0], ALU.subtract),
              reads=[ee_b], writes=[dd_b])
        for l in range(DEPTH):
            li = 0.8 - 0.6 * math.exp(-0.3 * l)
            S.add("dve", lambda e, l=l, li=li: e.tensor_scalar(neglam[:, l:l + 1], dd[:, l:l + 1], -li, None, ALU.add),
                  reads=[dd_b], writes=[neglam_b])
            S.add("act", lambda e, l=l, li=li: e.mul(gsub[:, l, :], sg[:, l * 128:(l + 1) * 128], 1.0 - li),
                  reads=[sg_b], writes=[gsub_b])
        S.add("act", lambda e: e.activation(esink[:], sk_[:], AF.Exp), reads=[sk_b], writes=[esink_b])
        return L.off

    def phase_rope(start):
        L = Layout(nc, start)
        posi, posi_b = L.t("posi", [128, SEQ], I32)
        tt_, tt_b = L.t("rt", [128, SEQ], F32)
        t2, t2_b = L.t("rt2", [128, SEQ], F32)
        ti, ti_b = L.t("rti", [128, SEQ], I32)
        tf, tf_b = L.t("rtf", [128, SEQ], F32)
        res = [L.t(f"rres{i}", [128, SEQ], F32) for i in range(2)]
        for s in range(NSEQ):
            S.add("sp", lambda e, s=s: [e.dma_start(out=posi[:, i * 512:(i + 1) * 512],
                                                    in_=pos_d[s:s + 1, i * 512:(i + 1) * 512].broadcast_to([128, 512])) for i in range(8)],
                  reads=[w_b], writes=[posi_b], dma=posi_b, ndma=8)
            S.add("dve", lambda e: e.tensor_copy(tt_[:], posi[:]), reads=[posi_b], writes=[tt_b])
            S.add("dve", lambda e: e.tensor_scalar(tt_[:], tt_[:], invt[:, 0:1], 1.0 / (2.0 * math.pi), ALU.mult, ALU.mult),
                  reads=[tt_b, invt_b], writes=[tt_b])
            for which in (1, 0):
                if which == 0:
                    S.add("dve", lambda e: e.tensor_scalar(t2[:], tt_[:], 0.25, None, ALU.add), reads=[tt_b], writes=[t2_b])
                    src, src_b = t2, t2_b
                else:
                    src, src_b = tt_, tt_b
                S.add("dve", lambda e, src=src: e.tensor_copy(ti[:], src[:]), reads=[src_b], writes=[ti_b])
                S.add("dve", lambda e: e.tensor_copy(tf[:], ti[:]), reads=[ti_b], writes=[tf_b])
                S.add("dve", lambda e, src=src: e.tensor_tensor(tf[:], src[:], tf[:], ALU.subtract),
                      reads=[src_b, tf_b], writes=[tf_b])
                r, r_b = res[which]
                S.add("act", lambda e, r=r: e.activation(r[:], tf[:], AF.Sin, scale=6.28318), reads=[tf_b], writes=[r_b])
                S.add("sp", lambda e, r=r, s=s, which=which: e.dma_start(out=ROPE[s, which], in_=r[:]),
                      reads=[r_b], writes=[ROPE_b[s]], dma=r_b)

    def ln_epilogue_a(y_ap, y_bufs, coef, xres, xres_b, z, z_b, st, st_b):
        S.add("dve", lambda e: e.scalar_tensor_tensor(z[:].rearrange("p (a b) -> p a b", a=2), y_ap, coef / ALPHA,
                                                      xres[:].rearrange("p (a b) -> p a b", a=2), ALU.mult, ALU.add),
              reads=list(y_bufs) + [xres_b], writes=[z_b])
        for hh in range(2):
            S.add("dve", lambda e, hh=hh: e.bn_stats(st[:, hh * 6:(hh + 1) * 6], z[:, hh * 512:(hh + 1) * 512]),
                  reads=[z_b], writes=[st_b])
        S.add("dve", lambda e: e.bn_aggr(st[:, 12:14], st[:, 0:12].rearrange("p (a b) -> p a b", a=2)),
              reads=[st_b], writes=[st_b])
        S.add("dve", lambda e: e.tensor_scalar(st[:, 14:15], st[:, 13:14], LN_EPS / (ALPHA * ALPHA), None, ALU.add),
              reads=[st_b], writes=[st_b])
        S.add("pool", lambda e: e.tensor_tensor(st[:, 15:16], st[:, 14:15], negh[:], ALU.pow),
              reads=[st_b, negh_b], writes=[st_b])

    def ln_epilogue_b1(z, z_b, st, st_b, zn_eng="dve"):
        S.add("dve", lambda e: e.scalar_tensor_tensor(st[:, 16:17], st[:, 12:13], -1.0, st[:, 15:16], ALU.mult, ALU.mult),
              reads=[st_b], writes=[st_b])
        if zn_eng == "dve":
            S.add("dve", lambda e: e.tensor_scalar(z[:], z[:], st[:, 15:16], st[:, 16:17], ALU.mult, ALU.add),
                  reads=[z_b, st_b], writes=[z_b])
        else:
            S.add("act", lambda e: e.activation(z[:], z[:], AF.Identity, bias=st[:, 16:17], scale=st[:, 15:16]),
                  reads=[z_b, st_b], writes=[z_b])

    def ln_epilogue_b2(z, z_b, dst_ap, dst_buf, is_out, g_eng="pool", b_eng="pool"):
        if g_eng == "dve":
            S.add("dve", lambda e: e.tensor_tensor(z[:, 0:640], z[:, 0:640], gb[:, 0, 0:640], ALU.mult), reads=[z_b, gb_b], writes=[z_b])
            S.add("pool", lambda e: e.tensor_tensor(z[:, 640:1024], z[:, 640:1024], gb[:, 0, 640:1024], ALU.mult), reads=[z_b, gb_b], writes=[z_b])
        else:
            S.add(g_eng, lambda e: e.tensor_tensor(z[:], z[:], gb[:, 0, :], ALU.mult), reads=[z_b, gb_b], writes=[z_b])
        S.add(b_eng, lambda e: e.tensor_tensor(z[:], z[:], gb[:, 1, :], ALU.add), reads=[z_b, gb_b], writes=[z_b])
        op = S.add("pool", lambda e: e.dma_start(out=dst_ap, in_=z[:]), reads=[z_b], writes=[dst_buf], dma=z_b)
        if is_out:
            out_stores.append(op)

    def ln_epilogue_b(z, z_b, st, st_b, dst_ap, dst_buf, is_out):
        ln_epilogue_b1(z, z_b, st, st_b)
        ln_epilogue_b2(z, z_b, dst_ap, dst_buf, is_out)

    def load_gb(l, i):
        S.add("sp", lambda e: [e.dma_start(out=gb[:, 0, :], in_=lng_d[l * 3 + i:l * 3 + i + 1, :].broadcast_to([128, D])),
                               e.dma_start(out=gb[:, 1, :], in_=lnb_d[l * 3 + i:l * 3 + i + 1, :].broadcast_to([128, D]))],
              reads=[w_b], writes=[gb_b], dma=gb_b, ndma=2)

    def xt_prologue(src, src_bufs, tt, xbf, xbf_b, xT, xT_b, tp_banks):
        S.add("pool", lambda e: e.dma_start(out=xbf[:], in_=src[tt * 512:(tt + 1) * 512, :].rearrange("(j p) d -> p j d", p=128)),
              reads=list(src_bufs[tt]), writes=[xbf_b], dma=xbf_b)

    def xt_transposes(xbf, xbf_b, xT, xT_b, tp_banks, js=(0, 1, 2, 3)):
        for j in js:
            bk = tp_banks[j % len(tp_banks)]
            for k in range(8):
                S.add("pe", lambda e, j=j, k=k, bk=bk: e.transpose(PSB[:, bk, k * 128:(k + 1) * 128],
                                                                 xbf[:, j, k * 128:(k + 1) * 128], ident[:]),
                      reads=[xbf_b, ident_b], writes=[psb[bk]])
            eng = "dve" if j % 2 == 0 else "act"
            if eng == "dve":
                S.add("dve", lambda e, j=j, bk=bk: e.tensor_copy(xT[:, :, j * 128:(j + 1) * 128],
                                                                 PSB[:, bk, :].rearrange("p (k t) -> p k t", k=8)),
                      reads=[psb[bk]], writes=[xT_b])
            else:
                S.add("act", lambda e, j=j, bk=bk: e.copy(xT[:, :, j * 128:(j + 1) * 128],
                                                          PSB[:, bk, :].rearrange("p (k t) -> p k t", k=8)),
                      reads=[psb[bk]], writes=[xT_b])

    def phase_ffn(l, which, src, src_bufs, dst, dst_bufs, is_out):
        S.barrier()
        wi_d = (f1i_d if which == 1 else f2i_d)
        wo_d = (f1o_d if which == 1 else f2o_d)
        L = Layout(nc, COMMON_END)
        wi, _ = L.t("wi", [128, 8, 2 * DFF], BF16)
        wi_b = [Buf(f"wi{g}") for g in range(11)]
        wo, wo_b = L.t("wo", [128, NF, D], BF16)
        xbf, xbf_b = L.t("xbf", [128, 4, D], BF16)
        xT, xT_b = L.t("xT", [128, 8, 512], BF16)
        gT, gT_b = L.t("gT", [128, NF, 512], BF16)
        sil = [L.t(f"sil{i}", [128, 512], F32) for i in range(2)]
        xres = [L.t(f"xres{i}", [128, D], F32) for i in range(2)]
        zz = [L.t(f"z{i}", [128, D], F32) for i in range(2)]
        wsrc = wi_d[l].rearrange("(k p) n -> p k n", p=128)
        for g in range(11):
            S.add("pool", lambda e, g=g: [
                e.dma_start(out=wi[:, :, g * 256:(g + 1) * 256], in_=wsrc[:, :, g * 256:(g + 1) * 256]),
                e.dma_start(out=wi[:, :, DFF + g * 256:DFF + (g + 1) * 256], in_=wsrc[:, :, DFF + g * 256:DFF + (g + 1) * 256])],
                reads=[w_b], writes=[wi_b[g]], dma=wi_b[g], ndma=2)
            if g == 0:
                xt_prologue(src, src_bufs, 0, xbf, xbf_b, xT, xT_b, None)
        wosrc = wo_d[l].rearrange("(f p) n -> p f n", p=128)
        S.add("pool", lambda e: [e.dma_start(out=wo[:, 0:11, :], in_=wosrc[:, 0:11, :]),
                                 e.dma_start(out=wo[:, 11:22, :], in_=wosrc[:, 11:22, :])],
              reads=[w_b], writes=[wo_b], dma=wo_b, ndma=2)
        load_gb(l, 0 if which == 1 else 2)
        xt_transposes(xbf, xbf_b, xT, xT_b, (0, 2))
        pend = [None]
        for tt in range(NTT):
            if tt + 1 < NTT:
                xt_prologue(src, src_bufs, tt + 1, xbf, xbf_b, xT, xT_b, None)
            for f in range(NF):
                hb = f % 2
                bg, bu = 2 * hb, 2 * hb + 1
                for k in range(8):
                    S.add("pe", lambda e, f=f, k=k, bg=bg: e.matmul(PS[:, bg, :], wi[:, k, f * 128:(f + 1) * 128], xT[:, k, :],
                                                                    start=(k == 0), stop=(k == 7)),
                          reads=[wi_b[f // 2], xT_b], writes=[psb[bg]])
                for k in range(8):
                    S.add("pe", lambda e, f=f, k=k, bu=bu: e.matmul(PS[:, bu, :], wi[:, k, DFF + f * 128:DFF + (f + 1) * 128], xT[:, k, :],
                                                                    start=(k == 0), stop=(k == 7)),
                          reads=[wi_b[f // 2], xT_b], writes=[psb[bu]])
                sl, sl_b = sil[hb]
                S.add("act", lambda e, sl=sl, bg=bg: e.activation(sl[:], PS[:, bg, :], AF.Silu), reads=[psb[bg]], writes=[sl_b])
                S.add("dve", lambda e, sl=sl, bu=bu, f=f: e.tensor_tensor(gT[:, f, :], PS[:, bu, :], sl[:], ALU.mult),
                      reads=[psb[bu], sl_b], writes=[gT_b])
            for j in range(4):
                yb = j % 2
                b0 = 4 + 2 * yb
                xr, xr_b = xres[yb]
                r0 = tt * 512 + j * 128
                S.add("sp", lambda e, xr=xr, r0=r0: e.dma_start(out=xr[:], in_=src[r0:r0 + 128, :]),
                      reads=[src_bufs[tt][j]], writes=[xr_b], dma=xr_b)
                for n in range(2):
                    for f in range(NF):
                        S.add("pe", lambda e, j=j, n=n, f=f, b0=b0: e.matmul(PS[:, b0 + n, :], gT[:, f, j * 128:(j + 1) * 128],
                                                                             wo[:, f, n * 512:(n + 1) * 512],
                                                                             start=(f == 0), stop=(f == NF - 1)),
                              reads=[gT_b, wo_b], writes=[psb[b0 + n]])
                if tt + 1 < NTT and j < 2:
                    xt_transposes(xbf, xbf_b, xT, xT_b, (0, 2), js=(2 * j, 2 * j + 1))
                z, z_b = zz[yb]
                st, st_b = stt[yb]
                ln_epilogue_a(PS[:, b0:b0 + 2, :], (psb[b0], psb[b0 + 1]), 0.5, xr, xr_b, z, z_b, st, st_b)
                ln_epilogue_b(z, z_b, st, st_b, dst[r0:r0 + 128, :], dst_bufs[tt][j], is_out)

    def phase_proj(l, src, src_bufs):
        S.barrier()
        L = Layout(nc, COMMON_END)
        wqk, wqk_b = L.t("wqk", [128, 8, 1792], BF16)
        wrot, wrot_b = L.t("wrot", [128, 8, 1792], BF16)
        wv, wv_b = L.t("wv", [128, 8, 640], BF16)
        xbf, xbf_b = L.t("xbf", [128, 4, D], BF16)
        xT, xT_b = L.t("xT", [128, 8, 512], BF16)
        cs = [L.t(f"cs{i}", [128, 2, 512], F32) for i in range(2)]
        qko = [L.t(f"qko{i}", [128, 14, 512], BF16) for i in range(2)]
        vo = [L.t(f"vo{i}", [128, 4, 640], BF16) for i in range(2)]
        t1 = [L.t(f"t1_{i}", [128, 512], F32) for i in range(2)]
        t2 = [L.t(f"t2_{i}", [128, 512], F32) for i in range(2)]
        wsrc = w_in_d[l].rearrange("(k p) n -> p k n", p=128)
        xt_prologue(src, src_bufs, 0, xbf, xbf_b, xT, xT_b, None)
        S.add("pool", lambda e: [
            e.dma_start(out=wqk[:, :, 0:1024], in_=wsrc[:, :, 0:1024]),
            e.dma_start(out=wqk[:, :, 1024:1664], in_=wsrc[:, :, 1536:2176]),
            e.dma_start(out=wqk[:, :, 1664:1728], in_=wsrc[:, :, 2112:2176]),
            e.dma_start(out=wqk[:, :, 1728:1792], in_=wsrc[:, :, 2048:2112])],
            reads=[w_b], writes=[wqk_b], dma=wqk_b, ndma=4)
        S.add("pool", lambda e: [
            e.dma_start(out=wv[:, :, 0:512], in_=wsrc[:, :, 1024:1536]),
            e.dma_start(out=wv[:, :, 512:640], in_=wsrc[:, :, 2176:2304])],
            reads=[w_b], writes=[wv_b], dma=wv_b, ndma=2)
        for k in range(8):
            a = wqk[:, k, :].rearrange("p (h w j) -> p h w j", w=2, j=32)
            b = wrot[:, k, :].rearrange("p (h w j) -> p h w j", w=2, j=32)
            S.add("act", lambda e, a=a, b=b: e.mul(b[:, :, 0, :], a[:, :, 1, :], -1.0), reads=[wqk_b], writes=[wrot_b])
            S.add("dve", lambda e, a=a, b=b: e.tensor_copy(b[:, :, 1, :], a[:, :, 0, :]), reads=[wqk_b], writes=[wrot_b])
        for tt in range(NTT):
            s = tt // 8
            p0 = (tt % 8) * 512
            pb = tt % 2
            c_, c_b = cs[pb]
            S.add("sp", lambda e, c_=c_, s=s, p0=p0: e.dma_start(out=c_[:], in_=ROPE[s].rearrange("c p t -> p c t")[:, :, p0:p0 + 512]),
                  reads=[ROPE_b[s]], writes=[c_b], dma=c_b)
            xt_transposes(xbf, xbf_b, xT, xT_b, (7,))
            if tt + 1 < NTT:
                xt_prologue(src, src_bufs, tt + 1, xbf, xbf_b, xT, xT_b, None)
            q, q_b = qko[pb]
            for c in range(14):
                ab = c % 2
                bm, br = ab, 2 + ab
                for k in range(8):
                    S.add("pe", lambda e, c=c, k=k, bm=bm: e.matmul(PS[:, bm, :], wqk[:, k, c * 128:(c + 1) * 128], xT[:, k, :],
                                                                    start=(k == 0), stop=(k == 7)),
                          reads=[wqk_b, xT_b], writes=[psb[bm]])
                for k in range(8):
                    S.add("pe", lambda e, c=c, k=k, br=br: e.matmul(PS[:, br, :], wrot[:, k, c * 128:(c + 1) * 128], xT[:, k, :],
                                                                    start=(k == 0), stop=(k == 7)),
                          reads=[wrot_b, xT_b], writes=[psb[br]])
                a1, a1_b = t1[ab]
                a2, a2_b = t2[ab]
                S.add("dve", lambda e, a1=a1, bm=bm, c_=c_: e.tensor_tensor(a1[:], PS[:, bm, :], c_[:, 0, :], ALU.mult),
                      reads=[psb[bm], c_b], writes=[a1_b])
                S.add("dve", lambda e, a2=a2, br=br, c_=c_: e.tensor_tensor(a2[:], PS[:, br, :], c_[:, 1, :], ALU.mult),
                      reads=[psb[br], c_b], writes=[a2_b])
                S.add("pool", lambda e, a1=a1, a2=a2, q=q, c=c: e.tensor_tensor(q[:, c, :], a1[:], a2[:], ALU.add),
                      reads=[a1_b, a2_b], writes=[q_b])
            S.add("sp", lambda e, q=q, tt=tt: e.dma_start(out=QKT.rearrange("c p t -> p c t")[:, :, tt * 512:(tt + 1) * 512], in_=q[:]),
                  reads=[q_b], writes=[QKT_b[c][tt] for c in range(14)], dma=q_b)
            v, v_b = vo[pb]
            for j in range(4):
                for k in range(8):
                    S.add("pe", lambda e, j=j, k=k: e.matmul(PS[:, 4, :], xT[:, k, j * 128:(j + 1) * 128], wv[:, k, 0:512],
                                                             start=(k == 0), stop=(k == 7)),
                          reads=[wv_b, xT_b], writes=[psb[4]])
                for k in range(8):
                    S.add("pe", lambda e, j=j, k=k: e.matmul(PS[:, 5, 0:128], xT[:, k, j * 128:(j + 1) * 128], wv[:, k, 512:640],
                                                             start=(k == 0), stop=(k == 7)),
                          reads=[wv_b, xT_b], writes=[psb[5]])
                S.add("act", lambda e, v=v, j=j: e.copy(v[:, j, 0:512], PS[:, 4, :]), reads=[psb[4]], writes=[v_b])
                S.add("act", lambda e, v=v, j=j: e.copy(v[:, j, 512:640], PS[:, 5, 0:128]), reads=[psb[5]], writes=[v_b])
            S.add("sp", lambda e, v=v, tt=tt: e.dma_start(out=VS[tt * 512:(tt + 1) * 512, :].rearrange("(j p) d -> p j d", p=128), in_=v[:]),
                  reads=[v_b], writes=[VS_b[tt]], dma=v_b)

    def phase_attn(l, src, src_bufs, dst, dst_bufs, is_out, do_mix=True):
        S.barrier()
        L = Layout(nc, COMMON_END)
        wo, wo_b = L.t("wmo", [128, 8, D], BF16)
        qth = [L.t(f"qth{i}", [128, SEQ], BF16) for i in range(2)]
        kth = [L.t(f"kth{i}", [128, SEQ], BF16) for i in range(2)]
        vau = [L.t(f"vau{i}", [128, 32, 130], BF16) for i in range(2)]
        pt = [L.t(f"pt{i}", [128, 1024], BF16) for i in range(4)]

        ostg, ostg_b = L.t("ostg", [128, 9, 130], F32)
        rall, rall_b = L.t("rall", [128, 24], F32)
        aall, aall_b = L.t("aall", [128, 4, 128], F32)
        ball, ball_b = L.t("ball", [128, 4, 128], F32)
        oall, oall_b = L.t("oall", [128, 4, 128], F32)
        yall, yall_b = L.t("yall", [128, 4, 128], BF16)
        yts = [L.t(f"yts{i}", [128, 512], BF16) for i in range(2)]
        sq, sq_b = L.t("sq", [128, 4, SEQ], BF16)
        skk, skk_b = L.t("skk", [128, 2, SEQ], BF16)
        vsw, vsw_b = L.t("vsw", [128, 32, 2, 66], BF16)
        mk, mk_b = L.t("mk", [128, 768], BF16)
        pts = [L.t(f"pts{i}", [128, 768], BF16) for i in range(4)]
        den = [L.t(f"den{i}", [128, 8], F32) for i in range(2)]
        ysw = [L.t(f"ysw{i}", [128, 256], BF16) for i in range(2)]
        ytsw = [L.t(f"ytsw{i}", [128, 4, 512], BF16) for i in range(2)]
        ytt, ytt_b = L.t("ytt", [128, 8, 512], BF16)
        xres = [L.t(f"xres{i}", [128, D], F32) for i in range(2)]
        zz = [L.t(f"z{i}", [128, D], F32) for i in range(5)]
        stl = [L.t(f"stl{i}", [128, 32], F32) for i in range(5)]

        S.add("pool", lambda e: e.dma_start(out=wo[:], in_=w_out_d[l].rearrange("(k p) n -> p k n", p=128)),
              reads=[w_b], writes=[wo_b], dma=wo_b)
        S.add("pool", lambda e: e.dma_start(out=mk[:], in_=mask_d), reads=[w_b], writes=[mk_b], dma=mk_b)
        for i in range(2):
            S.add("pool", lambda e, i=i: e.memset(vau[i][0][:, :, 128:130], 1.0), writes=[vau[i][1]])
        S.add("pool", lambda e: e.memset(vsw[:, :, :, 64:66], 1.0), writes=[vsw_b])
        load_gb(l, 1)

        OSLOT = [(4 + i // 3, (i % 3) * 130) for i in range(8)]

        def do_seq(s):
            T0 = s * SEQ
            tts = range(s * 8, s * 8 + 8)
            steps = []
            for h in range(4):
                for qi in range(8):
                    for kc in range(32):
                        steps.append((h, qi, kc))
            deferred = {}

            def load_head(h):
                hb = h % 2
                q, q_b = qth[hb]
                k_, k_b = kth[hb]
                v, v_b = vau[hb]
                S.add("sp", lambda e: e.dma_start(out=q[:], in_=QKT[h][:, T0:T0 + SEQ]),
                      reads=[QKT_b[h][t] for t in tts], writes=[q_b], dma=q_b)
                S.add("sp", lambda e: e.dma_start(out=k_[:], in_=QKT[4 + h][:, T0:T0 + SEQ]),
                      reads=[QKT_b[4 + h][t] for t in tts], writes=[k_b], dma=k_b)
                S.add("sp", lambda e: [e.dma_start(out=v[:, 8 * i:8 * i + 8, 0:128],
                                                   in_=VS[T0 + i * 1024:T0 + (i + 1) * 1024, h * 128:(h + 1) * 128].rearrange("(c p) d -> p c d", p=128))
                                       for i in range(4)],
                      reads=[VS_b[t] for t in tts], writes=[v_b], dma=v_b, ndma=4)

            def emit_qk(h, qi, kc, idx):
                hb = h % 2
                q, q_b = qth[hb]
                k_, k_b = kth[hb]
                sb = idx % 2
                p_, p_b = pt[idx % 4]
                S.add("pe", lambda e: e.matmul(PS[:, 2 * sb, :], k_[0:64, kc * 128:(kc + 1) * 128], q[0:64, qi * 512:(qi + 1) * 512],
                                               start=True, stop=True),
                      reads=[q_b, k_b], writes=[psb[2 * sb]])
                S.add("pe", lambda e: e.matmul(PS[:, 2 * sb + 1, :], k_[64:128, kc * 128:(kc + 1) * 128], q[64:128, qi * 512:(qi + 1) * 512],
                                               start=True, stop=True),
                      reads=[q_b, k_b], writes=[psb[2 * sb + 1]])
                S.add("act", lambda e: e.activation(p_[:].rearrange("p (a b) -> p a b", a=2), PS[:, 2 * sb:2 * sb + 2, :], AF.Exp, scale=SCALE),
                      reads=[psb[2 * sb], psb[2 * sb + 1]], writes=[p_b])

            def emit_pv(h, qi, kc, idx):
                hb = h % 2
                v, v_b = vau[hb]
                p_, p_b = pt[idx % 4]
                for i in range(8):
                    c, qs = i // 4, i % 4
                    bk, col = OSLOT[i]
                    first = (kc == 0) and (i % 3 == 0)
                    S.add("pe", lambda e, c=c, qs=qs, bk=bk, col=col, first=first: e.matmul(
                        PS[:, bk, col:col + 129], p_[:, c * 512 + qs * 128:c * 512 + (qs + 1) * 128], v[:, kc, 0:129],
                        start=first, stop=(kc == 31), skip_group_check=True),
                        reads=[p_b, v_b], writes=[psb[bk]])
                if FILLER:
                    q, q_b = qth[hb]
                    S.add("pe", lambda e: e.matmul(PS[:, 7, 256:384], ident[:], q[:, qi * 512:qi * 512 + 128], start=True, stop=True,
                                                   skip_group_check=True),
                          reads=[q_b, ident_b], writes=[psb[7]])

            def epi1(h, qi):
                for b in range(3):
                    S.add("dve", lambda e, b=b: e.tensor_copy(ostg[:, 3 * b:3 * b + 3, :].rearrange("p a b -> p (a b)"), PS[:, 4 + b, 0:390]),
                          reads=[psb[4 + b]], writes=[ostg_b])
                S.add("dve", lambda e: e.reciprocal(rall[:, 0:8], ostg[:, 0:8, 128]), reads=[ostg_b], writes=[rall_b])
                S.add("dve", lambda e: e.tensor_scalar(rall[:, 8:12], rall[:, 4:8], neglam[:, l:l + 1], None, ALU.mult),
                      reads=[rall_b, neglam_b], writes=[rall_b])
                S.add("dve", lambda e: e.tensor_tensor(aall[:], ostg[:, 0:4, 0:128], bcast_last(rall[:, 0:4], 128), ALU.mult),
                      reads=[ostg_b, rall_b], writes=[aall_b])
                S.add("dve", lambda e: e.tensor_tensor(ball[:], ostg[:, 4:8, 0:128], bcast_last(rall[:, 8:12], 128), ALU.mult),
                      reads=[ostg_b, rall_b], writes=[ball_b])
                S.add("dve", lambda e: e.tensor_tensor(oall[:], aall[:], ball[:], ALU.add), reads=[aall_b, ball_b], writes=[oall_b])

            def epi2(h, qi):
                yt_, yt_b = yts[qi % 2]
                S.add("dve", lambda e: e.tensor_tensor(aall[:], oall[:], oall[:], ALU.mult), reads=[oall_b], writes=[aall_b])
                S.add("dve", lambda e: e.reduce_sum(rall[:, 12:16], aall[:], axis=AX.X), reads=[aall_b], writes=[rall_b])
                S.add("dve", lambda e: e.tensor_scalar(rall[:, 16:20], rall[:, 12:16], 1.0 / 128.0, RMS_EPS, ALU.mult, ALU.add),
                      reads=[rall_b], writes=[rall_b])
                S.add("pool", lambda e: e.tensor_tensor(rall[:, 20:24], rall[:, 16:20], bcast_cols(negh[:, 0:1], 4), ALU.pow),
                      reads=[rall_b, negh_b], writes=[rall_b])
                S.add("dve", lambda e: e.tensor_tensor(ball[:], oall[:], bcast_last(rall[:, 20:24], 128), ALU.mult),
                      reads=[oall_b, rall_b], writes=[ball_b])
                gs = gsub[:, l, :]
                gsb = bass.AP(gs.tensor, gs.offset, [list(gs.ap[0]), [0, 4], list(gs.ap[-1])])
                S.add("dve", lambda e: e.tensor_tensor(yall[:], ball[:], gsb, ALU.mult), reads=[ball_b, gsub_b], writes=[yall_b])
                for qs in range(4):
                    S.add("pe", lambda e, qs=qs: e.transpose(PSB[:, 7, qs * 128:(qs + 1) * 128], yall[:, qs, :], ident[:]),
                          reads=[yall_b, ident_b], writes=[psb[7]])
                S.add("dve", lambda e: e.tensor_copy(yt_[:], PSB[:, 7, 0:512]), reads=[psb[7]], writes=[yt_b])
                tq = s * 8 + qi
                S.add("pool", lambda e: e.dma_start(out=YT[h][:, T0 + qi * 512:T0 + (qi + 1) * 512], in_=yt_[:]),
                      reads=[yt_b], writes=[YT_b[h][tq]], dma=yt_b)

            n = len(steps)
            load_head(0)
            LAG = 2
            for i in range(n + LAG):
                if i < n:
                    h, qi, kc = steps[i]
                    emit_qk(h, qi, kc, i)
                    if qi == 1 and kc == 0 and h + 1 < 4:
                        load_head(h + 1)
                if i >= LAG:
                    h, qi, kc = steps[i - LAG]
                    emit_pv(h, qi, kc, i - LAG)
                    if kc == 31:
                        epi1(h, qi)
                        deferred.setdefault(i + 12, []).append((h, qi))
                for (hh, qq) in deferred.pop(i, []):
                    epi2(hh, qq)
            for key in sorted(deferred):
                for (hh, qq) in deferred[key]:
                    epi2(hh, qq)

            S.add("sp", lambda e: [e.dma_start(out=sq[:, c, :], in_=QKT[8 + c][:, T0:T0 + SEQ]) for c in range(4)],
                  reads=[QKT_b[8 + c][t] for c in range(4) for t in tts], writes=[sq_b], dma=sq_b, ndma=4)
            S.add("sp", lambda e: [e.dma_start(out=skk[:, c, :], in_=QKT[12 + c][:, T0:T0 + SEQ]) for c in range(2)],
                  reads=[QKT_b[12 + c][t] for c in range(2) for t in tts], writes=[skk_b], dma=skk_b, ndma=2)
            S.add("sp", lambda e: [e.dma_start(out=vsw[:, 8 * i:8 * i + 8, g, 0:64],
                                               in_=VS[T0 + i * 1024:T0 + (i + 1) * 1024, 512 + g * 64:512 + (g + 1) * 64].rearrange("(c p) d -> p c d", p=128))
                                   for g in range(2) for i in range(4)],
                  reads=[VS_b[t] for t in tts], writes=[vsw_b], dma=vsw_b, ndma=8)
            def swa_a(blk, g, it):
                dlo = -1 if blk > 0 else 0
                dhi = 1 if blk < 31 else 0
                bufi = it % 2
                plo, plo_b = pts[2 * bufi]
                phi, phi_b = pts[2 * bufi + 1]
                klo_c = 0 if g == 0 else 1
                khi_c = 1 if g == 0 else 0
                for hl in range(2):
                    for dl_ in range(dlo, dhi + 1):
                        di = dl_ + 1
                        bi = hl * 3 + di
                        kb = blk + dl_
                        S.add("pe", lambda e, hl=hl, bi=bi, kb=kb: e.matmul(
                            PS[:, bi // 4, (bi % 4) * 128:(bi % 4 + 1) * 128],
                            skk[0:64, klo_c, kb * 128:(kb + 1) * 128], sq[0:64, 2 * g + hl, blk * 128:(blk + 1) * 128],
                            start=True, stop=True),
                            reads=[skk_b, sq_b], writes=[psb[bi // 4]])
                        S.add("pe", lambda e, hl=hl, bi=bi, kb=kb: e.matmul(
                            PS[:, 2 + bi // 4, (bi % 4) * 128:(bi % 4 + 1) * 128],
                            skk[64:128, khi_c, kb * 128:(kb + 1) * 128], sq[64:128, 2 * g + hl, blk * 128:(blk + 1) * 128],
                            start=True, stop=True),
                            reads=[skk_b, sq_b], writes=[psb[2 + bi // 4]])
                S.add("act", lambda e: e.activation(plo[:], PS[:, 0:2, :].rearrange("p a b -> p (a b)")[:, 0:768], AF.Exp, scale=SCALE),
                      reads=[psb[0], psb[1]], writes=[plo_b])
                S.add("act", lambda e: e.activation(phi[:], PS[:, 2:4, :].rearrange("p a b -> p (a b)")[:, 0:768], AF.Exp, scale=SCALE),
                      reads=[psb[2], psb[3]], writes=[phi_b])
                S.add("dve", lambda e: e.tensor_tensor(plo[:], plo[:], mk[:], ALU.mult), reads=[plo_b, mk_b], writes=[plo_b])
                S.add("dve", lambda e: e.tensor_tensor(phi[:, 0:256], phi[:, 0:256], mk[:, 0:256], ALU.mult), reads=[phi_b, mk_b], writes=[phi_b])
                S.add("pool", lambda e: e.tensor_tensor(phi[:, 256:768], phi[:, 256:768], mk[:, 256:768], ALU.mult), reads=[phi_b, mk_b], writes=[phi_b])

            def swa_b(blk, g, it):
                dlo = -1 if blk > 0 else 0
                dhi = 1 if blk < 31 else 0
                bufi = it % 2
                plo, plo_b = pts[2 * bufi]
                phi, phi_b = pts[2 * bufi + 1]
                ob_ = 4 + 2 * (it % 2)
                firstmm = True
                for hd in range(4):
                    hl, half = hd // 2, hd % 2
                    p_, p_b = (plo, plo_b) if half == 0 else (phi, phi_b)
                    for dl_ in range(dlo, dhi + 1):
                        di = dl_ + 1
                        bi = hl * 3 + di
                        kb = blk + dl_
                        S.add("pe", lambda e, hd=hd, bi=bi, kb=kb, p_=p_, firstmm=firstmm, last=(dl_ == dhi and hd == 3): e.matmul(
                            PS[:, ob_, hd * 66:hd * 66 + 65], p_[:, bi * 128:(bi + 1) * 128], vsw[:, kb, g, 0:65],
                            start=firstmm, stop=last, skip_group_check=True),
                            reads=[p_b, vsw_b], writes=[psb[ob_]])
                        firstmm = False
                dn, dn_b = den[bufi]
                yw, yw_b = ysw[bufi]
                o3 = PS[:, ob_, 0:264].rearrange("p (h d) -> p h d", d=66)
                S.add("dve", lambda e: e.tensor_tensor(dn[:, 0:4], o3[:, :, 64], esink[:, l * 8 + 4 * g:l * 8 + 4 * g + 4], ALU.add),
                      reads=[psb[ob_], esink_b], writes=[dn_b])
                S.add("dve", lambda e: e.reciprocal(dn[:, 4:8], dn[:, 0:4]), reads=[dn_b], writes=[dn_b])
                S.add("dve", lambda e: e.tensor_tensor(yw[:].rearrange("p (h d) -> p h d", d=64), o3[:, :, 0:64],
                                                       bcast_last(dn[:, 4:8], 64), ALU.mult),
                      reads=[psb[ob_], dn_b], writes=[yw_b])

            def swa_c(blk, g, it):
                bufi = it % 2
                yw, yw_b = ysw[bufi]
                ys_, ys_b = ytsw[(blk // 4) % 2]
                for cc in range(2):
                    S.add("pe", lambda e, cc=cc: e.transpose(PSB[:, 5, cc * 128:(cc + 1) * 128], yw[:, cc * 128:(cc + 1) * 128], ident[:]),
                          reads=[yw_b, ident_b], writes=[psb[5]])
                S.add("dve", lambda e: e.tensor_copy(ys_[:, 2 * g:2 * g + 2, (blk % 4) * 128:(blk % 4 + 1) * 128],
                                                     PSB[:, 5, 0:256].rearrange("p (c t) -> p c t", c=2)),
                      reads=[psb[5]], writes=[ys_b])
                if blk % 4 == 3 and g == 1:
                    tq = s * 8 + blk // 4
                    S.add("pool", lambda e: e.dma_start(
                        out=YT[4:8].rearrange("c p t -> p c t")[:, :, tq * 512:(tq + 1) * 512], in_=ys_[:]),
                        reads=[ys_b], writes=[YT_b[4 + c][tq] for c in range(4)], dma=ys_b)

            its = [(blk, g) for blk in range(32) for g in range(2)]
            for it in range(len(its) + 2):
                if it < len(its):
                    swa_a(its[it][0], its[it][1], it)
                if 1 <= it <= len(its):
                    swa_b(its[it - 1][0], its[it - 1][1], it - 1)
                if it >= 2:
                    swa_c(its[it - 2][0], its[it - 2][1], it - 2)

            if not do_mix:
                return
            pend = [None]
            pend2 = [None]
            for tq in tts:
                S.add("sp", lambda e, tq=tq: e.dma_start(out=ytt[:], in_=YT.rearrange("c p t -> p c t")[:, :, tq * 512:(tq + 1) * 512]),
                      reads=[YT_b[c][tq] for c in range(8)], writes=[ytt_b], dma=ytt_b)
                for j in range(4):
                    yb = j % 2
                    b0 = 4 + 2 * yb
                    xr, xr_b = xres[yb]
                    r0 = tq * 512 + j * 128
                    S.add("sp", lambda e, xr=xr, r0=r0: e.dma_start(out=xr[:], in_=src[r0:r0 + 128, :]),
                          reads=[src_bufs[tq][j]], writes=[xr_b], dma=xr_b)
                    for n_ in range(2):
                        for c in range(8):
                            S.add("pe", lambda e, j=j, n_=n_, c=c, b0=b0: e.matmul(PS[:, b0 + n_, :], ytt[:, c, j * 128:(j + 1) * 128],
                                                                                 wo[:, c, n_ * 512:(n_ + 1) * 512],
                                                                                 start=(c == 0), stop=(c == 7)),
                                  reads=[ytt_b, wo_b], writes=[psb[b0 + n_]])
                    zi = (tq * 4 + j) % 5
                    z, z_b = zz[zi]
                    st, st_b = stl[zi]
                    ln_epilogue_a(PS[:, b0:b0 + 2, :], (psb[b0], psb[b0 + 1]), 1.0, xr, xr_b, z, z_b, st, st_b)
                    if pend[0] is not None:
                        p_ = pend[0]
                        ln_epilogue_b1(p_[0], p_[1], p_[2], p_[3], zn_eng="act")
                    if pend2[0] is not None:
                        p_ = pend2[0]
                        ln_epilogue_b2(p_[0], p_[1], p_[4], p_[5], p_[6], g_eng="dve", b_eng="pool")
                    pend2[0] = pend[0]
                    pend[0] = (z, z_b, st, st_b, dst[r0:r0 + 128, :], dst_bufs[tq][j], is_out)
            if pend[0] is not None:
                p_ = pend[0]
                ln_epilogue_b1(p_[0], p_[1], p_[2], p_[3], zn_eng="act")
            if pend2[0] is not None:
                p_ = pend2[0]
                ln_epilogue_b2(p_[0], p_[1], p_[4], p_[5], p_[6], g_eng="dve", b_eng="pool")
            if pend[0] is not None:
                p_ = pend[0]
                ln_epilogue_b2(p_[0], p_[1], p_[4], p_[5], p_[6], g_eng="dve", b_eng="pool")
            pend[0] = None
            pend2[0] = None

        for s_ in range(NSEQ):
            do_seq(s_)

    phase_rope(phase_scalars())
    done = False
    if isinstance(stop, list):
        cur, cur_b = x_in, XIN_b
        for i, ph in enumerate(stop):
            lastp = (i == len(stop) - 1)
            d_, d_b = (out_d, OUT_b) if lastp else (XS, XS_b)
            if ph == "ffn1":
                phase_ffn(0, 1, cur, cur_b, d_, d_b, lastp)
            elif ph == "ffn2":
                phase_ffn(0, 2, cur, cur_b, d_, d_b, lastp)
            elif ph == "proj":
                phase_proj(0, cur, cur_b)
                continue
            elif ph == "attn":
                phase_attn(0, cur, cur_b, d_, d_b, lastp)
            cur, cur_b = XS, XS_b
        depth = 0
    for l in range(0 if stop != "rope" else depth, depth):
        last_layer = (l == depth - 1)
        src, src_bufs = (x_in, XIN_b) if l == 0 else (XS, XS_b)
        fin = stop == (l, "ffn1")
        phase_ffn(l, 1, src, src_bufs, out_d if fin else XS, OUT_b if fin else XS_b, fin)
        if fin:
            break
        if stop == (l, "proj"):
            phase_proj(l, XS, XS_b)
            break
        phase_proj(l, XS, XS_b)
        if stop == (l, "attn"):
            phase_attn(l, XS, XS_b, XS, XS_b, False, do_mix=False)
            break
        fin = stop == (l, "mix")
        phase_attn(l, XS, XS_b, out_d if fin else XS, OUT_b if fin else XS_b, fin)
        if fin:
            break
        fin = last_layer or stop == (l, "ffn2")
        phase_ffn(l, 2, XS, XS_b, out_d if fin else XS, OUT_b if fin else XS_b, fin)
        if fin:
            break
    S.add("sp", None, extra=list(out_stores) + list(S.dmas_since))
    S.emit()
    S.close()
    return nc


def make_consts():
    ident = np.eye(128, dtype=np.float32)
    inv = np.power(np.float32(10000.0), -(np.arange(0, 64, 2, dtype=np.float32) / np.float32(64))).astype(np.float32)
    inv_tab = np.tile(inv, 4).reshape(128, 1).astype(np.float32)
    j = np.arange(128)[:, None]
    i = np.arange(128)[None, :]
    m = np.ones((128, 2, 3, 128), dtype=np.float32)
    m[:, :, 0, :] = (j >= i)[:, None, :]
    m[:, :, 2, :] = (j <= i)[:, None, :]
    return ident, inv_tab, np.ascontiguousarray(m.reshape(128, 768))


def make_in_maps(inputs, nseq, cores, wdepth=DEPTH):
    ident, inv_tab, mask = make_consts()
    x = np.ascontiguousarray(np.asarray(inputs["x"], dtype=np.float32))
    pos = np.ascontiguousarray(np.asarray(inputs["positions"], dtype=np.int32))
    shared = {
        "w_in": np.ascontiguousarray(np.asarray(inputs["w_in"], dtype=np.float32)[:wdepth]),
        "w_out": np.ascontiguousarray(np.asarray(inputs["w_out"], dtype=np.float32)[:wdepth]),
        "diff_lambda": np.asarray(inputs["diff_lambda"], dtype=np.float32).reshape(1, -1),
        "diff_subln_g": np.asarray(inputs["diff_subln_g"], dtype=np.float32).reshape(1, -1),
        "swa_sink": np.asarray(inputs["swa_sink"], dtype=np.float32).reshape(1, -1),
        "ffn1_w_in": np.ascontiguousarray(np.asarray(inputs["ffn1_w_in"], dtype=np.float32)[:wdepth]),
        "ffn1_w_out": np.ascontiguousarray(np.asarray(inputs["ffn1_w_out"], dtype=np.float32)[:wdepth]),
        "ffn2_w_in": np.ascontiguousarray(np.asarray(inputs["ffn2_w_in"], dtype=np.float32)[:wdepth]),
        "ffn2_w_out": np.ascontiguousarray(np.asarray(inputs["ffn2_w_out"], dtype=np.float32)[:wdepth]),
        "ln_g": np.asarray(inputs["ln_g"], dtype=np.float32).reshape(DEPTH * 3, D),
        "ln_b": np.asarray(inputs["ln_b"], dtype=np.float32).reshape(DEPTH * 3, D),
        "c_ident": ident, "c_inv": inv_tab, "c_mask": mask,
    }
    maps = []
    for c in cores:
        m = dict(shared)
        m["x"] = x[c * nseq:(c + 1) * nseq].reshape(nseq * SEQ, D)
        m["positions"] = pos[c * nseq:(c + 1) * nseq]
        maps.append(m)
    return maps


def kernel(**inputs):
    nseq = 2
    nc = build(NSEQ=nseq)
    maps = make_in_maps(inputs, nseq, list(range(NCORES)))
    res = run_bass_kernel_spmd(nc, maps, core_ids=list(range(NCORES)))
    outs = [np.asarray(r["out"], dtype=np.float32).reshape(nseq, SEQ, D) for r in res.results]
    return np.concatenate(outs, axis=0)
```
